# Optimizing a Trainium2 kernel written in Bass

```python
import jax, jax.numpy as jnp
from jax import lax
import numpy as np

D_MODEL = 2048
BATCH = 4
SEQ = 4096
DEPTH = 1

MIX_WIDTH = D_MODEL
CONV_WIDTH = MIX_WIDTH // 2
CONV_GROUPS = 8
CONV_K = 3
DN_HEADS = 8
DN_HEAD_DIM = 128
DN_WIDTH = DN_HEADS * DN_HEAD_DIM
DN_CONV_K = 4
CHUNK = 64
D_FF = 5632
FFN_CONV_K = 3
PLE_DIM = 256
EPS = 1e-6
IN_COLS = 3 * CONV_WIDTH + 4 * DN_WIDTH + 2 * DN_HEADS

kernel_name = "hybrid_shortconv_gated_deltanet_convffn_ple"


def rmsnorm(x, g):
    xf = x.astype(jnp.float32)
    y = xf * lax.rsqrt(jnp.mean(xf * xf, axis=-1, keepdims=True) + EPS) * g.astype(jnp.float32)
    return y.astype(x.dtype)


def causal_dwconv(x, w):
    K = w.shape[0]
    S = x.shape[1]
    xp = jnp.pad(x, ((0, 0), (K - 1, 0), (0, 0)))
    y = xp[:, 0:S] * w[0]
    for j in range(1, K):
        y = y + xp[:, j:j + S] * w[j]
    return y


def l2norm(x):
    return x * lax.rsqrt(jnp.sum(x * x, axis=-1, keepdims=True) + EPS)


def chunk_gated_delta(q, k, v, g, beta):
    B, H, S, dk = q.shape
    dv = v.shape[-1]
    N = S // CHUNK
    q = q * (dk ** -0.5)
    qc = q.reshape(B, H, N, CHUNK, dk)
    kc = k.reshape(B, H, N, CHUNK, dk)
    vc = v.reshape(B, H, N, CHUNK, dv)
    bc = beta.reshape(B, H, N, CHUNK)
    gcum = jnp.cumsum(g.reshape(B, H, N, CHUNK), axis=-1)
    idx = jnp.arange(CHUNK)
    causal = idx[:, None] >= idx[None, :]
    strict = idx[:, None] > idx[None, :]
    diff = gcum[..., :, None] - gcum[..., None, :]
    decay = jnp.exp(jnp.where(causal, diff, -jnp.inf))
    kk = jnp.einsum('bhncd,bhnmd->bhncm', kc, kc)
    L = jnp.where(strict, kk * decay * bc[..., :, None], 0.0)
    A = L + jnp.eye(CHUNK, dtype=jnp.float32)
    rhs = jnp.concatenate([vc * bc[..., None],
                           kc * (bc * jnp.exp(gcum))[..., None]], axis=-1)
    sol = lax.linalg.triangular_solve(A, rhs, left_side=True, lower=True)
    u = sol[..., :dv]
    w = sol[..., dv:]
    qk = jnp.einsum('bhncd,bhnmd->bhncm', qc, kc) * decay
    q_dec = qc * jnp.exp(gcum)[..., None]
    k_dec = kc * jnp.exp(gcum[..., -1:] - gcum)[..., None]
    g_last = jnp.exp(gcum[..., -1])

    def step(state, inp):
        u_n, w_n, qk_n, qd_n, kd_n, gl_n = inp
        v_new = u_n - jnp.einsum('bhcd,bhde->bhce', w_n, state)
        o = (jnp.einsum('bhcd,bhde->bhce', qd_n, state)
             + jnp.einsum('bhcm,bhme->bhce', qk_n, v_new))
        state = state * gl_n[..., None, None] + jnp.einsum('bhcd,bhce->bhde', kd_n, v_new)
        return state, o

    to_front = lambda t: jnp.moveaxis(t, 2, 0)
    xs = (to_front(u), to_front(w), to_front(qk), to_front(q_dec), to_front(k_dec),
          jnp.moveaxis(g_last, 2, 0))
    s0 = jnp.zeros((B, H, dk, dv), jnp.float32)
    _, o = lax.scan(step, s0, xs)
    return jnp.moveaxis(o, 0, 2).reshape(B, H, S, dv)


def hybrid_layer(x, p_i, norm_mix_g, w_in, conv_a_w, conv_qkv_w, a_log, dt_bias, dn_norm_g,
                 w_out, norm_ffn_g, w_up, conv_ffn_w, w_down, norm_ple_g, w_ple_gate, w_ple_proj):
    Bsz, S, _ = x.shape
    h = rmsnorm(x, norm_mix_g)
    proj = h @ w_in
    s1 = CONV_WIDTH
    s2 = 2 * CONV_WIDTH
    s3 = 3 * CONV_WIDTH
    s4 = s3 + 3 * DN_WIDTH
    s5 = s4 + DN_WIDTH
    s6 = s5 + DN_HEADS
    a_x, a_b, a_c, qkv, z, a_dec, b_beta = jnp.split(proj, [s1, s2, s3, s4, s5, s6], axis=-1)

    y_a = a_b * causal_dwconv(a_c * a_x, conv_a_w)

    qkv = jax.nn.silu(causal_dwconv(qkv, conv_qkv_w)).astype(jnp.float32)
    q, k, v = jnp.split(qkv, 3, axis=-1)
    q = l2norm(q.reshape(Bsz, S, DN_HEADS, DN_HEAD_DIM))
    k = l2norm(k.reshape(Bsz, S, DN_HEADS, DN_HEAD_DIM))
    v = v.reshape(Bsz, S, DN_HEADS, DN_HEAD_DIM)
    g = -jnp.exp(a_log.astype(jnp.float32)) * jax.nn.softplus(
        a_dec.astype(jnp.float32) + dt_bias.astype(jnp.float32))
    beta = jax.nn.sigmoid(b_beta.astype(jnp.float32))
    tr = lambda t: jnp.swapaxes(t, 1, 2)
    o = chunk_gated_delta(tr(q), tr(k), tr(v), tr(g), tr(beta))
    o = tr(o)
    zf = z.astype(jnp.float32).reshape(Bsz, S, DN_HEADS, DN_HEAD_DIM)
    o = (o * lax.rsqrt(jnp.mean(o * o, axis=-1, keepdims=True) + EPS)
         * dn_norm_g.astype(jnp.float32) * jax.nn.silu(zf))
    y_b = o.reshape(Bsz, S, DN_WIDTH).astype(x.dtype)

    x = x + jnp.concatenate([y_a, y_b], axis=-1) @ w_out

    h = rmsnorm(x, norm_ffn_g)
    up = causal_dwconv(h @ w_up, conv_ffn_w)
    gate, val = jnp.split(up, 2, axis=-1)
    x = x + (jax.nn.silu(gate) * val) @ w_down

    ple_gate = jax.nn.sigmoid(rmsnorm(x, norm_ple_g) @ w_ple_gate)
    x = x + ple_gate * (p_i @ w_ple_proj)
    return x


def setup_inputs(seed: int = 0) -> dict:
    key = jax.random.key(seed)
    ks = jax.random.split(key, 20)
    f32 = jnp.float32
    nrm = lambda k, shape, scale: jax.random.normal(k, shape, f32) * scale
    gain = lambda k, shape: 1.0 + 0.02 * jax.random.normal(k, shape, f32)
    return {
        "x": jax.random.normal(ks[0], (BATCH, SEQ, D_MODEL), f32),
        "p": jax.random.normal(ks[1], (DEPTH, BATCH, SEQ, PLE_DIM), f32),
        "norm_mix_g": gain(ks[2], (DEPTH, D_MODEL)),
        "w_in": nrm(ks[3], (DEPTH, D_MODEL, IN_COLS), D_MODEL ** -0.5),
        "conv_a_w": nrm(ks[4], (DEPTH, CONV_K, CONV_WIDTH), CONV_K ** -0.5),
        "conv_qkv_w": nrm(ks[5], (DEPTH, DN_CONV_K, 3 * DN_WIDTH), DN_CONV_K ** -0.5),
        "a_log": jnp.log(jax.random.uniform(ks[6], (DEPTH, DN_HEADS), f32, 1.0, 16.0)),
        "dt_bias": 0.1 * jax.random.normal(ks[7], (DEPTH, DN_HEADS), f32),
        "dn_norm_g": gain(ks[8], (DEPTH, DN_HEAD_DIM)),
        "w_out": nrm(ks[9], (DEPTH, MIX_WIDTH, D_MODEL), MIX_WIDTH ** -0.5),
        "norm_ffn_g": gain(ks[10], (DEPTH, D_MODEL)),
        "w_up": nrm(ks[11], (DEPTH, D_MODEL, 2 * D_FF), D_MODEL ** -0.5),
        "conv_ffn_w": nrm(ks[12], (DEPTH, FFN_CONV_K, 2 * D_FF), FFN_CONV_K ** -0.5),
        "w_down": nrm(ks[13], (DEPTH, D_FF, D_MODEL), D_FF ** -0.5),
        "norm_ple_g": gain(ks[14], (DEPTH, D_MODEL)),
        "w_ple_gate": nrm(ks[15], (DEPTH, D_MODEL, D_MODEL), D_MODEL ** -0.5),
        "w_ple_proj": nrm(ks[16], (DEPTH, PLE_DIM, D_MODEL), PLE_DIM ** -0.5),
        "final_norm_g": gain(ks[17], (D_MODEL,)),
    }


def reference(x, p, norm_mix_g, w_in, conv_a_w, conv_qkv_w, a_log, dt_bias, dn_norm_g,
              w_out, norm_ffn_g, w_up, conv_ffn_w, w_down, norm_ple_g, w_ple_gate,
              w_ple_proj, final_norm_g):
    for i in range(DEPTH):
        x = hybrid_layer(x, p[i], norm_mix_g[i], w_in[i], conv_a_w[i], conv_qkv_w[i],
                         a_log[i], dt_bias[i], dn_norm_g[i], w_out[i], norm_ffn_g[i],
                         w_up[i], conv_ffn_w[i], w_down[i], norm_ple_g[i], w_ple_gate[i],
                         w_ple_proj[i])
    return rmsnorm(x, final_norm_g)
```

```python
import numpy as np
from contextlib import ExitStack
import concourse.bass as bass
import concourse.mybir as mybir
from concourse.bass_utils import run_bass_kernel_spmd

F32 = mybir.dt.float32
BF16 = mybir.dt.bfloat16
AF = mybir.ActivationFunctionType
ALU = mybir.AluOpType

ENGS = ("pe", "act", "dve", "pool", "sp")
NDMASEM = 24


class Op:
    __slots__ = ("eng", "fn", "deps", "sig", "sigval", "idx", "dma", "dsem", "dval", "prev_on_sem")

    def __init__(self, eng, fn, dma):
        self.eng = eng
        self.fn = fn
        self.deps = []
        self.sig = False
        self.sigval = 0
        self.idx = 0
        self.dma = dma
        self.dsem = None
        self.dval = 0
        self.prev_on_sem = None


class Prog:
    def __init__(self, nc, es):
        self.nc = nc
        self.es = es
        self.streams = {e: [] for e in ENGS}
        self.last_w = {}
        self.readers = {}
        self.esem = {e: es.enter_context(nc.semaphore("s_" + e)) for e in ENGS if e != "sp"}
        self.dsems = {}
        self.dcount = {}
        self.dlast = {}
        self.dn = {e: 0 for e in ENGS}

    def _dma_sem(self, eng):
        i = self.dn[eng] % NDMASEM
        self.dn[eng] += 1
        key = (eng, i)
        if key not in self.dsems:
            self.dsems[key] = self.es.enter_context(self.nc.semaphore("d_%s_%d" % (eng, i)))
            self.dcount[key] = 0
            self.dlast[key] = None
        return key

    def op(self, eng, fn, r=(), w=(), dma=False):
        o = Op(eng, fn, dma)
        st = self.streams[eng]
        o.idx = len(st)
        best = {}

        def add(d, raw):
            if d is None:
                return
            if d.dma:
                best[("d", id(d))] = d
                return
            if d.eng == eng:
                if eng == "pe":
                    return
            k = d.eng
            if k not in best or best[k].idx < d.idx:
                best[k] = d

        for t in r:
            add(self.last_w.get(t), True)
        for t in w:
            add(self.last_w.get(t), False)
            for rd in self.readers.get(t, ()):
                add(rd, False)
        o.deps = list(best.values())
        for d in o.deps:
            if not d.dma:
                d.sig = True
        for t in r:
            self.readers.setdefault(t, []).append(o)
        for t in w:
            self.last_w[t] = o
            self.readers[t] = []
        if dma:
            key = self._dma_sem(eng)
            o.dsem = key
            o.prev_on_sem = self.dlast[key]
            self.dcount[key] += 16
            o.dval = self.dcount[key]
            self.dlast[key] = o
        st.append(o)
        return o

    def emit(self):
        nc = self.nc
        for e in ENGS:
            c = 0
            for o in self.streams[e]:
                if o.sig:
                    c += 1
                    o.sigval = c
        streams = self.streams
        esem = self.esem
        dsems = self.dsems

        def run(ename, eng):
            seen = {}
            for o in streams[ename]:
                waits = []
                for d in o.deps:
                    if d.dma:
                        waits.append((dsems[d.dsem], d.dval, d.dsem))
                    else:
                        waits.append((esem[d.eng], d.sigval, d.eng))
                if o.dma and o.prev_on_sem is not None:
                    p = o.prev_on_sem
                    waits.append((dsems[p.dsem], p.dval, p.dsem))
                for sem, val, key in waits:
                    if seen.get(key, 0) >= val:
                        continue
                    seen[key] = val
                    eng.wait_ge(sem, val)
                ins = o.fn(eng)
                if o.dma:
                    ins.then_inc(dsems[o.dsem], 16)
                elif o.sig:
                    ins.then_inc(esem[ename], 1)

        with nc.Block() as block:
            @block.tensor
            def _(eng):
                run("pe", eng)

            @block.scalar
            def _(eng):
                run("act", eng)

            @block.vector
            def _(eng):
                run("dve", eng)

            @block.gpsimd
            def _(eng):
                run("pool", eng)

            @block.sync
            def _(eng):
                run("sp", eng)


D = 2048
KC = 16
T = 704
NT = 352
NCH = 11
C = 64
NPASS_PRE = 3
NPASS_MAIN = 3
SEQ = T * (NPASS_PRE + NPASS_MAIN)
NMAIN = T * NPASS_MAIN
HALO = 64
NOUT = 2048
DFF = 5632
NHC = 44
GRP = 11
NG = 4
EPS = 1e-6
NSLOT = 6
NWK = 8
NCG = 6
GROUPS = ((0, 6), (6, 11))
HEADS0 = (0, 2, 4, 6)
HEADS1 = (1, 3, 5, 7)
STAGGER = 20

PO = {}
_o = 0
for _n, _w in (("g1", 16), ("g2", 16), ("g3", 16), ("gf", 16), ("ca", 24), ("cq", 96), ("cf", 264),
               ("dng", 1), ("dtb", 8), ("alog", 8), ("ident", 128), ("mus", 64), ("mui", 64), ("mls", 64)):
    PO[_n] = _o
    _o += _w
NPAR = _o


def build_nc(debug=False):
    nc = bass.Bass("TRN2", target_bir_lowering=False)
    dt = nc.dram_tensor
    xin = dt("xin", [KC, 128, SEQ], F32, kind="ExternalInput").ap()
    pin = dt("pin", [2, 128, NMAIN], F32, kind="ExternalInput").ap()
    par = dt("par", [128, NPAR], F32, kind="ExternalInput").ap()
    wA = dt("wA", [24, 128, KC, 128], F32, kind="ExternalInput").ap()
    wH = dt("wH", [32, 128, KC, 128], F32, kind="ExternalInput").ap()
    wab = dt("wab", [128, KC, 16], F32, kind="ExternalInput").ap()
    wO = dt("wO", [16, 128, KC, 128], F32, kind="ExternalInput").ap()
    wU = dt("wU", [88, 128, KC, 128], F32, kind="ExternalInput").ap()
    wD = dt("wD", [NG * 16, 128, GRP, 128], F32, kind="ExternalInput").ap()
    wG = dt("wG", [16, 128, KC, 128], F32, kind="ExternalInput").ap()
    wP = dt("wP", [16, 128, 2, 128], F32, kind="ExternalInput").ap()
    out = dt("out", [KC, 128, NOUT], F32, kind="ExternalOutput").ap()

    with ExitStack() as es:
        P = Prog(nc, es)
        sb = lambda name, shape, dtype: es.enter_context(nc.sbuf_tensor(name, shape, dtype))

        parS = sb("parS", [128, NPAR], F32)
        x = sb("x", [128, KC, T], F32)
        h = sb("h", [128, KC, T], BF16)
        y = sb("y", [128, KC, T], BF16)
        wslot = [sb("ws%d" % i, [128, KC, 128], BF16) for i in range(NSLOT)]
        wabS = sb("wabS", [128, KC, 16], BF16)
        wk = [sb("wk%d" % i, [128, T + 4], F32) for i in range(NWK)]
        sqb = [sb("sqb%d" % i, [128, T], BF16) for i in range(2)]
        identb = sb("identb", [128, 128], BF16)
        onesD = sb("onesD", [128, 128], BF16)
        onesH = sb("onesH", [128, 128], BF16)
        ones1 = sb("ones1", [128, 128], BF16)
        onesF = sb("onesF", [64, 128], F32)
        epsT = sb("epsT", [128, 1], F32)
        hista = sb("hista", [128, 8, 2], F32)
        histq = sb("histq", [128, 24, 3], F32)
        histf = sb("histf", [128, 88, 2], F32)
        pSb = sb("pSb", [128, 2, T], BF16)
        NG8 = NCH * 8
        gt = [sb("gt%d" % i, [64, NCH, 8], F32) for i in range(3)]
        gsb = sb("gsb", [64, NCH, 8], F32)
        gcS = sb("gcS", [64, NCH, 8], F32)
        betaS = sb("betaS", [64, NCH, 8], F32)
        eklS = sb("eklS", [64, NCH, 8], F32)
        nbkS = sb("nbkS", [64, NCH, 8], F32)
        eglS = sb("eglS", [128, NCH, 8], F32)
        negA = sb("negA", [64, 8], F32)
        S32 = sb("S32", [128, 8, 128], F32)
        Sbf = sb("Sbf", [128, 8, 128], BF16)

        class Z:
            pass
        GW = NCG * C
        SC = []
        for s_ in range(2):
            z_ = Z()
            n_ = lambda nm: "%s_%d" % (nm, s_)
            z_.qs = sb(n_("qs"), [128, T], BF16)
            z_.qd = sb(n_("qd"), [128, T], BF16)
            z_.kT = sb(n_("kT"), [128, T], BF16)
            z_.vT = sb(n_("vT"), [128, T], BF16)
            z_.zs = sb(n_("zs"), [128, T], BF16)
            z_.tm = [sb(n_("tm%d" % i), [128, GW], F32) for i in range(2)]
            z_.eg = sb(n_("eg"), [128, GW], BF16)
            z_.gcrow = sb(n_("gcrow"), [128, GW], F32)
            z_.E1u = sb(n_("E1u"), [64, NCG, C], BF16)
            z_.E1i = sb(n_("E1i"), [64, NCG, C], BF16)
            z_.E2l = sb(n_("E2l"), [64, NCG, C], BF16)
            z_.Xb = [sb(n_("Xb%d" % i), [64, NCG, C], BF16) for i in range(2)]
            z_.Yb = [sb(n_("Yb%d" % i), [64, NCG, C], BF16) for i in range(2)]
            z_.Qb = [sb(n_("Qb%d" % i), [64, NCG, C], BF16) for i in range(2)]
            z_.QKm = sb(n_("QKm"), [64, NCG, C], BF16)
            z_.kbg = sb(n_("kbg"), [64, NCG, 128], BF16)
            z_.kdec = sb(n_("kdec"), [64, NCG, 128], BF16)
            z_.vb = sb(n_("vb"), [64, NCG, 128], BF16)
            z_.wTn = sb(n_("wTn"), [128, GW], BF16)
            z_.vnew = [sb(n_("vnew%d" % i), [64, 128], BF16) for i in range(2)]
            SC.append(z_)
        ps = [es.enter_context(nc.psum_tensor("ps%d" % i, [128, 2, 512], F32)) for i in range(4)]
        psf = [p_[:].rearrange("p a b -> p (a b)") for p_ in ps]

        def PT(i):
            return [("P", i, 0), ("P", i, 1)]

        def act(out_, in_, func, r, w, bias=0.0, scale=1.0):
            return P.op("act", lambda e: e.activation(out=out_, in_=in_, func=func, bias=bias, scale=scale), r, w)

        def rsq(out_, in_, r, w):
            act(out_, in_, AF.Ln, list(r) + ["epsT"], w, bias=epsT[:, 0:1])
            act(out_, out_, AF.Exp, w, w, scale=-0.5)

        def tt(eng, out_, in0, in1, op, r, w):
            return P.op(eng, lambda e: e.tensor_tensor(out=out_, in0=in0, in1=in1, op=op), r, w)

        def ts(eng, out_, in0, s1, s2, op0, op1, r, w):
            return P.op(eng, lambda e: e.tensor_scalar(out=out_, in0=in0, scalar1=s1, scalar2=s2, op0=op0, op1=op1), r, w)

        def ts1(eng, out_, in0, s1, op0, r, w):
            return P.op(eng, lambda e: e.tensor_single_scalar(out=out_, in_=in0, scalar=s1, op=op0), r, w)

        def stt(eng, out_, in0, sc, in1, op0, op1, r, w):
            return P.op(eng, lambda e: e.scalar_tensor_tensor(out=out_, in0=in0, scalar=sc, in1=in1, op0=op0, op1=op1), r, w)

        def cp(eng, out_, in_, r, w):
            return P.op(eng, lambda e: e.tensor_copy(out=out_, in_=in_), r, w)

        def mm(out_, lhsT, rhs, start, stop, r, w):
            return P.op("pe", lambda e: e.matmul(out_, lhsT=lhsT, rhs=rhs, start=start, stop=stop), r, w)

        def dma(eng, out_, in_, r, w):
            return P.op(eng, lambda e: e.dma_start(out=out_, in_=in_), r, w, dma=True)

        def pcol(name, j=0, n=1):
            o = PO[name] + j
            return parS[:, o:o + n]

        wkfree = list(range(NWK))

        def getwk():
            assert wkfree, "wk pool exhausted"
            i = wkfree.pop(0)
            return wk[i], ("wk", i), i

        def freewk(i):
            assert i not in wkfree
            wkfree.append(i)

        sqn = [0]

        def getsq():
            i = sqn[0] % 2
            sqn[0] += 1
            return sqb[i], ("sq", i)

        bigp = [0]
        BIGB = [(0, 0), (0, 1), (1, 0), (1, 1), (3, 0), (3, 1)]

        def bigbank():
            i, j = BIGB[bigp[0] % len(BIGB)]
            bigp[0] += 1
            return ps[i][:, j, 0:NT], ("P", i, j)

        wsn = [0]

        def load_w(src, kc):
            s = wsn[0] % NSLOT
            wsn[0] += 1
            dma("pool", wslot[s][:, 0:kc, :], src, [], [("ws", s)])
            return wslot[s], ("ws", s)

        dma("sp", parS[:], par, [], ["par"])
        dma("pool", wabS[:], wab, [], ["wab"])
        cp("dve", identb[:], pcol("ident", 0, 128), ["par"], ["identb"])
        P.op("dve", lambda e: e.memset(onesD[:], 1.0 / D), [], ["onesD"])
        P.op("dve", lambda e: e.memset(onesH[:], 1.0 / 128), [], ["onesH"])
        P.op("dve", lambda e: e.memset(ones1[:], 1.0), [], ["ones1"])
        P.op("dve", lambda e: e.memset(onesF[:], 1.0), [], ["onesF"])
        P.op("dve", lambda e: e.memset(epsT[:], EPS), [], ["epsT"])
        P.op("pool", lambda e: e.memset(hista[:], 0.0), [], ["hista"])
        P.op("pool", lambda e: e.memset(histq[:], 0.0), [], [("histq", b_) for b_ in range(24)])
        P.op("pool", lambda e: e.memset(histf[:], 0.0), [], [("histf", b_) for b_ in range(88)])
        P.op("pool", lambda e: e.memset(S32[:], 0.0), [], [("S", hh) for hh in range(8)])
        P.op("pool", lambda e: e.memset(Sbf[:], 0.0), [], [("Sb", hh) for hh in range(8)])
        act(negA[:], parS[0:64, PO["alog"]:PO["alog"] + 8], AF.Exp, ["par"], ["negA0"])
        ts1("dve", negA[:], negA[:], -1.0, ALU.mult, ["negA0"], ["negA"])

        outtags = []

        def rmsnorm(gname, final_tok=None):
            st = ps[2]
            for fc in range(KC):
                sq, sqt = getsq()
                act(sq[:], x[:, fc, :], AF.Square, [("x", fc)], [sqt])
                for n in range(2):
                    mm(st[:, n, 0:NT], onesD[:], sq[:, n * NT:(n + 1) * NT], fc == 0, fc == KC - 1,
                       [sqt, "onesD"], [("P", 2, n)])
            rstd, rstt, rsti = getwk()
            for n in range(2):
                rsq(rstd[:, n * NT:(n + 1) * NT], st[:, n, 0:NT], [("P", 2, n)], [rstt])
            for fc in range(KC):
                if final_tok is None:
                    stt("dve", h[:, fc, :], x[:, fc, :], pcol(gname, fc), rstd[:, 0:T], ALU.mult, ALU.mult,
                        [("x", fc), rstt, "par"], [("h", fc)])
                else:
                    o0, skip = final_tok
                    wt, wtag, wi = getwk()
                    stt("dve", wt[:, 0:T], x[:, fc, :], pcol(gname, fc), rstd[:, 0:T], ALU.mult, ALU.mult,
                        [("x", fc), rstt, "par"], [wtag])
                    dma("sp", out[fc, :, o0:o0 + T - skip], wt[:, skip:T], [wtag], [("out", fc, o0)])
                    outtags.append(("out", fc, o0))
                    freewk(wi)
            freewk(rsti)

        def proj_to(wsrc, kc, act_src, act_tag, evac, banks=None):
            wt, wtag = load_w(wsrc, kc)
            pbs = []
            for n in range(2):
                if banks is None:
                    pbs.append(bigbank())
                else:
                    pbs.append((ps[banks][:, n, 0:NT], ("P", banks, n)))
            for k in range(kc):
                for n in range(2):
                    pb, ptag = pbs[n]
                    mm(pb, wt[:, k, :], act_src[:, k, n * NT:(n + 1) * NT], k == 0, k == kc - 1,
                       [wtag, (act_tag, k)], [ptag])
            for n in range(2):
                evac(n, pbs[n][0], pbs[n][1])

        def conv_stage(blk_hist, hist_tag, H, wname, wj, K):
            stg, stag, stgi = getwk()
            act(stg[:, 0:H], blk_hist, AF.Copy, [hist_tag], [stag])

            def finish():
                act(blk_hist, stg[:, T:T + H], AF.Copy, [stag], [hist_tag])
                c0, c0t, c0i = getwk()
                act(c0[:, 0:T], stg[:, H:H + T], AF.Copy, [stag, "par"], [c0t], scale=pcol(wname, wj + K - 1))
                for j in range(K - 2, -1, -1):
                    sh = H - (K - 1 - j)
                    stt("dve", c0[:, 0:T], stg[:, sh:sh + T], pcol(wname, wj + j), c0[:, 0:T], ALU.mult, ALU.add,
                        [stag, c0t, "par"], [c0t])
                freewk(stgi)
                return c0, c0t, c0i

            return stg, stag, finish

        def phase_A():
            for j in range(8):
                axs, axt, axi = getwk()

                def ev_ax(n, pb, ptag):
                    act(axs[:, n * NT:(n + 1) * NT], pb, AF.Copy, [ptag], [axt])
                proj_to(wA[3 * j + 0], KC, h, "h", ev_ax)
                stg, stag, finish = conv_stage(hista[:, j, :], "hista", 2, "ca", 3 * j, 3)

                def ev_ac(n, pb, ptag):
                    tt("dve", stg[:, 2 + n * NT:2 + (n + 1) * NT], pb, axs[:, n * NT:(n + 1) * NT], ALU.mult,
                       [ptag, axt], [stag])
                proj_to(wA[3 * j + 1], KC, h, "h", ev_ac)
                freewk(axi)
                cv, cvt, cvi = finish()

                def ev_ab(n, pb, ptag):
                    tt("dve", y[:, j, n * NT:(n + 1) * NT], pb, cv[:, n * NT:(n + 1) * NT], ALU.mult,
                       [ptag, cvt], [("y", j)])
                proj_to(wA[3 * j + 2], KC, h, "h", ev_ab)
                freewk(cvi)

        def gating():
            pg = psf[3]
            pgv = pg[0:64, 0:NCH * 16].rearrange("p (c s) -> p c s", s=16)
            for c in range(NCH):
                for k in range(KC):
                    mm(pgv[:, c, :], h[:, k, c * C:(c + 1) * C], wabS[:, k, :], k == 0, k == KC - 1,
                       [("h", k), "wab"], [("P", 3, 0)])
            dtb = parS[0:64, PO["dtb"]:PO["dtb"] + 8].unsqueeze(1).to_broadcast([64, NCH, 8])
            tt("dve", gt[0][:], pgv[:, :, 0:8], dtb, ALU.add, [("P", 3, 0), "par"], ["gt0"])
            act(gt[1][:], gt[0][:], AF.Exp, ["gt0"], ["gt1"])
            act(gt[2][:], gt[1][:], AF.Ln, ["gt1"], ["gt2"], bias=1.0)
            tt("dve", gsb[:], gt[2][:], negA[:].unsqueeze(1).to_broadcast([64, NCH, 8]), ALU.mult,
               ["gt2", "negA"], ["gsb"])
            act(betaS[:], pgv[:, :, 8:16], AF.Sigmoid, [("P", 3, 0)], ["beta"])
            gflat = gsb[:].rearrange("p c s -> p (c s)")
            pc = psf[3][0:64, 512:512 + NG8]
            pl = psf[2][:, 0:NG8]
            mm(pc, parS[0:64, PO["mui"]:PO["mui"] + 64], gflat, True, True, ["gsb", "par"], [("P", 3, 1)])
            mm(pl, onesF[:], gflat, True, True, ["gsb", "onesF"], [("P", 2, 0)])
            pc3 = pc.rearrange("p (c s) -> p c s", s=8)
            pl3 = pl.rearrange("p (c s) -> p c s", s=8)
            cp("dve", gcS[:], pc3, [("P", 3, 1)], ["gcS"])
            act(eglS[:], pl3, AF.Exp, [("P", 2, 0)], ["eglS"])
            tt("dve", gt[0][:], pl3[0:64], gcS[:], ALU.subtract, [("P", 2, 0), "gcS"], ["gt0"])
            act(eklS[:], gt[0][:], AF.Exp, ["gt0"], ["eklS"])
            act(gt[1][:], gcS[:], AF.Exp, ["gcS"], ["gt1"])
            stt("dve", nbkS[:], betaS[:], -1.0, gt[1][:], ALU.mult, ALU.mult, ["beta", "gt1"], ["nbkS"])

        def head(hh, main, s):
            zz = SC[s]
            iA, iB = 2 * s, 2 * s + 1
            PA, PB = ps[iA], ps[iB]
            PAf, PBf = psf[iA], psf[iB]
            A0, A1, B0, B1 = ("P", iA, 0), ("P", iA, 1), ("P", iB, 0), ("P", iB, 1)
            tg = lambda nm: (nm, s)
            order = (("q", 0), ("k", 1), ("v", 2)) if main else (("k", 1), ("v", 2))
            for nm, qi in order:
                blk = qi * 8 + hh
                stg, stag, finish = conv_stage(histq[:, blk, :], ("histq", blk), 3, "cq", 4 * blk, 4)

                def ev(n, pb, ptag, stg=stg, stag=stag):
                    act(stg[:, 3 + n * NT:3 + (n + 1) * NT], pb, AF.Copy, [ptag], [stag])
                proj_to(wH[4 * hh + qi], KC, h, "h", ev, banks=iA)
                yield
                cv, cvt, cvi = finish()
                if nm == "v":
                    act(zz.vT[:], cv[:, 0:T], AF.Silu, [cvt], [tg("vT")])
                    freewk(cvi)
                    continue
                act(cv[:, 0:T], cv[:, 0:T], AF.Silu, [cvt], [cvt])
                sq, sqt = getsq()
                act(sq[:], cv[:, 0:T], AF.Square, [cvt], [sqt])
                for n in range(2):
                    mm(PB[:, n, 0:NT], ones1[:], sq[:, n * NT:(n + 1) * NT], True, True, [sqt, "ones1"], [("P", iB, n)])
                yield
                rn, rnt, rni = getwk()
                for n in range(2):
                    rsq(rn[:, n * NT:(n + 1) * NT], PB[:, n, 0:NT], [("P", iB, n)], [rnt])
                if nm == "k":
                    tt("dve", zz.kT[:], cv[:, 0:T], rn[:, 0:T], ALU.mult, [cvt, rnt], [tg("kT")])
                else:
                    stt("dve", zz.qs[:], cv[:, 0:T], 128.0 ** -0.5, rn[:, 0:T], ALU.mult, ALU.mult,
                        [cvt, rnt], [tg("qs")])
                freewk(cvi)
                freewk(rni)
            if main:
                def ev_z(n, pb, ptag):
                    act(zz.zs[:, n * NT:(n + 1) * NT], pb, AF.Silu, [ptag], [tg("zs")])
                proj_to(wH[4 * hh + 3], KC, h, "h", ev_z, banks=iA)
                yield

            for (c0, c1) in GROUPS:
                n_ = c1 - c0
                W = n_ * C
                t0_, t1_ = c0 * C, c1 * C
                v3 = lambda ap: ap.rearrange("p (c j) -> p c j", j=C)
                id3 = parS[0:64, PO["ident"]:PO["ident"] + 64].unsqueeze(1).to_broadcast([64, n_, C])
                gch = gcS[:, c0:c1, hh:hh + 1].to_broadcast([64, n_, C])
                bth = betaS[:, c0:c1, hh:hh + 1].to_broadcast([64, n_, C])
                tmA = zz.tm[0][0:64, 0:W]
                tmB = zz.tm[1][0:64, 0:W]
                tt("dve", v3(tmA), id3, gch, ALU.mult, ["par", "gcS"], [tg("tmA")])
                tt("dve", v3(tmB), id3, bth, ALU.mult, ["par", "beta"], [tg("tmB")])
                mm(PBf[:, 0:W], onesF[:], tmA, True, True, [tg("tmA"), "onesF"], [B0])
                mm(PBf[0:64, 512:512 + W], onesF[:, 0:64], tmB, True, True, [tg("tmB"), "onesF"], [B1])
                yield
                act(zz.gcrow[:, 0:W], PBf[:, 0:W], AF.Copy, [B0], [tg("gcrow")])
                if main:
                    act(zz.eg[:, 0:W], PBf[:, 0:W], AF.Exp, [B0], [tg("eg")])
                    tt("dve", zz.qd[:, t0_:t1_], zz.qs[:, t0_:t1_], zz.eg[:, 0:W], ALU.mult,
                       [tg("qs"), tg("eg")], [tg("qd")])
                brow3 = v3(PBf[0:64, 512:512 + W])
                tm3 = v3(tmA)
                gr3 = v3(zz.gcrow[0:64, 0:W])
                tt("dve", tm3, gr3, gch, ALU.subtract, [tg("gcrow"), "gcS"], [tg("tmA")])
                m13 = v3(tmB)
                ts1("dve", m13, tm3, 0.0, ALU.min, [tg("tmA")], [tg("tmB")])
                act(m13, m13, AF.Exp, [tg("tmB")], [tg("tmB")])
                mus = parS[0:64, PO["mus"]:PO["mus"] + 64].unsqueeze(1).to_broadcast([64, n_, C])
                mui = parS[0:64, PO["mui"]:PO["mui"] + 64].unsqueeze(1).to_broadcast([64, n_, C])
                mls = parS[0:64, PO["mls"]:PO["mls"] + 64].unsqueeze(1).to_broadcast([64, n_, C])
                E1u = zz.E1u[:, 0:n_, :]
                E1i = zz.E1i[:, 0:n_, :]
                E2l = zz.E2l[:, 0:n_, :]
                tt("dve", E1u, m13, mus, ALU.mult, [tg("tmB"), "par"], [tg("E1u")])
                if main:
                    tt("dve", E1i, m13, mui, ALU.mult, [tg("tmB"), "par"], [tg("E1i")])
                ts("dve", tm3, tm3, -1.0, 0.0, ALU.mult, ALU.min, [tg("tmA")], [tg("tmA")])
                act(tm3, tm3, AF.Exp, [tg("tmA")], [tg("tmA")])
                tt("dve", E2l, tm3, mls, ALU.mult, [tg("tmA"), "par"], [tg("E2l")])
                pkk = v3(PAf[0:64, 0:W])
                for c in range(c0, c1):
                    kc_ = zz.kT[:, c * C:(c + 1) * C]
                    mm(pkk[:, c - c0, :], kc_, kc_, True, True, [tg("kT")], [A0])
                if main:
                    pqk = v3(PAf[0:64, 512:512 + W])
                    for c in range(c0, c1):
                        mm(pqk[:, c - c0, :], zz.kT[:, c * C:(c + 1) * C], zz.qs[:, c * C:(c + 1) * C], True, True,
                           [tg("kT"), tg("qs")], [A1])
                yield
                X = [xb[:, 0:n_, :] for xb in zz.Xb]
                Y = [yb[:, 0:n_, :] for yb in zz.Yb]
                Q = [qb[:, 0:n_, :] for qb in zz.Qb]
                Xt = [tg("X0"), tg("X1")]
                Yt = [tg("Y0"), tg("Y1")]
                Qt = [tg("Q0"), tg("Q1")]
                if main:
                    tt("dve", zz.QKm[:, 0:n_, :], pqk, E1i, ALU.mult, [A1, tg("E1i")], [tg("QKm")])
                tt("dve", m13, pkk, E1u, ALU.mult, [A0, tg("E1u")], [tg("tmB")])
                tt("dve", X[0], m13, brow3, ALU.mult, [tg("tmB"), B1], [Xt[0]])
                tt("dve", tm3, pkk, E2l, ALU.mult, [A0, tg("E2l")], [tg("tmA")])
                tt("dve", Y[0], tm3, bth, ALU.mult, [tg("tmA"), "beta"], [Yt[0]])
                tt("dve", Q[0], id3, X[0], ALU.subtract, ["par", Xt[0]], [Qt[0]])
                pY = v3(PBf[0:64, 0:W])
                pX = v3(PAf[0:64, 0:W])
                pQ = v3(PBf[0:64, 512:512 + W])
                for r_ in range(1, 7):
                    if r_ <= 5:
                        a, b = (r_ - 1) % 2, r_ % 2
                        for c in range(n_):
                            mm(pY[:, c, :], X[a][:, c, :], Y[a][:, c, :], True, True, [Xt[a], Yt[a]], [B0])
                        if r_ < 5:
                            for c in range(n_):
                                mm(pX[:, c, :], Y[a][:, c, :], X[a][:, c, :], True, True, [Xt[a], Yt[a]], [A0])
                    if r_ >= 2:
                        lv = r_ - 1
                        qa, qb = (lv - 1) % 2, lv % 2
                        for c in range(n_):
                            mm(pQ[:, c, :], Y[qb][:, c, :], Q[qa][:, c, :], True, True, [Yt[qb], Qt[qa]], [B1])
                    yield
                    if r_ >= 2:
                        tt("dve", Q[qb], pQ, Q[qa], ALU.add, [B1, Qt[qa]], [Qt[qb]])
                    if r_ <= 5:
                        cp("dve", Y[b], pY, [B0], [Yt[b]])
                        if r_ < 5:
                            act(X[b], pX, AF.Copy, [A0], [Xt[b]])
                TT_ = Q[1]
                TTt = Qt[1]
                pk = PAf[0:64, 0:n_ * 128]
                pv_ = PBf[0:64, 0:n_ * 128]
                for c in range(c0, c1):
                    cc = c - c0
                    mm(pk[:, cc * 128:(cc + 1) * 128], zz.kT[:, c * C:(c + 1) * C], identb[:], True, True,
                       [tg("kT"), "identb"], [A0, A1])
                    mm(pv_[:, cc * 128:(cc + 1) * 128], zz.vT[:, c * C:(c + 1) * C], identb[:], True, True,
                       [tg("vT"), "identb"], [B0, B1])
                yield
                pk3 = pk.rearrange("p (c d) -> p c d", d=128)
                pv3 = pv_.rearrange("p (c d) -> p c d", d=128)
                nb_ = nbkS[:, c0:c1, hh:hh + 1].to_broadcast([64, n_, 128])
                ek_ = eklS[:, c0:c1, hh:hh + 1].to_broadcast([64, n_, 128])
                bt_ = betaS[:, c0:c1, hh:hh + 1].to_broadcast([64, n_, 128])
                kbg = zz.kbg[:, 0:n_, :]
                kdec = zz.kdec[:, 0:n_, :]
                vb = zz.vb[:, 0:n_, :]
                tt("dve", kbg, pk3, nb_, ALU.mult, [A0, A1, "nbkS"], [tg("kbg")])
                tt("dve", kdec, pk3, ek_, ALU.mult, [A0, A1, "eklS"], [tg("kdec")])
                tt("dve", vb, pv3, bt_, ALU.mult, [B0, B1, "beta"], [tg("vb")])
                pw = PAf[:, 0:W]
                for cc in range(n_):
                    mm(pw[:, cc * C:(cc + 1) * C], kbg[:, cc, :], TT_[:, cc, :], True, True, [tg("kbg"), TTt], [A0])
                yield
                act(zz.wTn[:, 0:W], pw, AF.Copy, [A0], [tg("wTn")])
                po = PAf[:, 512:512 + W]
                for c in range(c0, c1):
                    cc = c - c0
                    pvn = PB[0:64, 0, (c % 2) * 128:(c % 2) * 128 + 128]
                    mm(pvn, TT_[:, cc, :], vb[:, cc, :], True, False, [TTt, tg("vb")], [B0])
                    mm(pvn, zz.wTn[:, cc * C:(cc + 1) * C], Sbf[:, hh, :], False, True, [tg("wTn"), ("Sb", hh)], [B0])
                    yield
                    vn = zz.vnew[c % 2]
                    vnt = tg("vnew%d" % (c % 2))
                    act(vn[:], pvn, AF.Copy, [B0], [vnt])
                    if main:
                        mm(po[:, cc * C:(cc + 1) * C], Sbf[:, hh, :], zz.qd[:, c * C:(c + 1) * C], True, False,
                           [("Sb", hh), tg("qd")], [A1])
                        mm(po[:, cc * C:(cc + 1) * C], vn[:], zz.QKm[:, cc, :], False, True, [vnt, tg("QKm")], [A1])
                    pSn = PB[:, 1, (c % 2) * 128:(c % 2) * 128 + 128]
                    mm(pSn, kdec[:, cc, :], vn[:], True, True, [tg("kdec"), vnt], [B1])
                    yield
                    stt("dve", Sbf[:, hh, :], S32[:, hh, :], eglS[:, c, hh:hh + 1], pSn, ALU.mult, ALU.add,
                        [("S", hh), "eglS", B1], [("Sb", hh)])
                    stt("dve", S32[:, hh, :], S32[:, hh, :], eglS[:, c, hh:hh + 1], pSn, ALU.mult, ALU.add,
                        [("S", hh), "eglS", B1], [("S", hh)])
                if not main:
                    continue
                sq, sqt = getsq()
                act(sq[:, 0:W], po, AF.Square, [A1], [sqt])
                mm(PAf[:, 0:W], onesH[:], sq[:, 0:W], True, True, [sqt, "onesH"], [A0])
                yield
                rn, rnt, rni = getwk()
                rsq(rn[:, 0:W], PAf[:, 0:W], [A0], [rnt])
                og, ogt, ogi = getwk()
                stt("dve", og[:, 0:W], po, pcol("dng"), rn[:, 0:W], ALU.mult, ALU.mult, [A1, rnt, "par"], [ogt])
                tt("dve", y[:, 8 + hh, t0_:t1_], og[:, 0:W], zz.zs[:, t0_:t1_], ALU.mult, [ogt, tg("zs")], [("y", 8 + hh)])
                freewk(rni)
                freewk(ogi)

        def run_heads(main):
            def chain(s, hs):
                for hh in hs:
                    yield from head(hh, main, s)
            gens = [chain(0, HEADS0), chain(1, HEADS1)]
            alive = [True, True]
            for _ in range(STAGGER):
                try:
                    next(gens[0])
                except StopIteration:
                    alive[0] = False
                    break
            while any(alive):
                for i in range(2):
                    if alive[i]:
                        try:
                            next(gens[i])
                        except StopIteration:
                            alive[i] = False

        def add_x(fo):
            def ev(n, pb, ptag):
                tt("dve", x[:, fo, n * NT:(n + 1) * NT], x[:, fo, n * NT:(n + 1) * NT], pb, ALU.add,
                   [("x", fo), ptag], [("x", fo)])
            return ev

        def phase_out():
            for fo in range(KC):
                proj_to(wO[fo], KC, y, "y", add_x(fo))

        def phase_ffn():
            for g in range(NG):
                for jj in range(GRP):
                    j = g * GRP + jj
                    res = []
                    for half in range(2):
                        blk = 2 * j + half
                        stg, stag, finish = conv_stage(histf[:, blk, :], ("histf", blk), 2, "cf", 3 * blk, 3)

                        def ev(n, pb, ptag, stg=stg, stag=stag):
                            act(stg[:, 2 + n * NT:2 + (n + 1) * NT], pb, AF.Copy, [ptag], [stag])
                        proj_to(wU[blk], KC, h, "h", ev)
                        res.append(finish())
                    (gc_, gct, gci), (vc_, vct, vci) = res
                    act(gc_[:, 0:T], gc_[:, 0:T], AF.Silu, [gct], [gct])
                    tt("dve", y[:, jj, :], gc_[:, 0:T], vc_[:, 0:T], ALU.mult, [gct, vct], [("y", jj)])
                    freewk(gci)
                    freewk(vci)
                for fo in range(KC):
                    proj_to(wD[g * 16 + fo], GRP, y, "y", add_x(fo))

        def phase_ple(mp):
            for i in range(2):
                pt_, ptt, pti = getwk()
                dma("sp", pt_[:, 0:T], pin[i, :, mp * T:(mp + 1) * T], [], [ptt])
                cp("dve", pSb[:, i, :], pt_[:, 0:T], [ptt], [("p", i)])
                freewk(pti)
            for fo in range(KC):
                sg, sgt, sgi = getwk()

                def ev_g(n, pb, ptag):
                    act(sg[:, n * NT:(n + 1) * NT], pb, AF.Sigmoid, [ptag], [sgt])
                proj_to(wG[fo], KC, h, "h", ev_g)

                def ev_p(n, pb, ptag):
                    tt("dve", sg[:, n * NT:(n + 1) * NT], sg[:, n * NT:(n + 1) * NT], pb, ALU.mult, [sgt, ptag], [sgt])
                    tt("dve", x[:, fo, n * NT:(n + 1) * NT], x[:, fo, n * NT:(n + 1) * NT], sg[:, n * NT:(n + 1) * NT],
                       ALU.add, [("x", fo), sgt], [("x", fo)])
                proj_to(wP[fo], 2, pSb, "p", ev_p)
                freewk(sgi)

        for ps_i in range(NPASS_PRE + NPASS_MAIN):
            main = ps_i >= NPASS_PRE
            t0 = ps_i * T
            for fc in range(KC):
                dma("sp", x[:, fc, :], xin[fc, :, t0:t0 + T], [], [("x", fc)])
            rmsnorm("g1")
            if main:
                phase_A()
            gating()
            run_heads(main)
            if not main:
                continue
            mp = ps_i - NPASS_PRE
            phase_out()
            rmsnorm("g2")
            phase_ffn()
            rmsnorm("g3")
            phase_ple(mp)
            if mp == 0:
                rmsnorm("gf", final_tok=(0, HALO))
            else:
                rmsnorm("gf", final_tok=(mp * T - HALO, 0))
        P.op("sp", lambda e: e.nop(), r=outtags)
        P.emit()
    return nc


def _blk(w, cols):
    K = w.shape[0]
    sub = w[:, cols]
    return np.ascontiguousarray(sub.reshape(K // 128, 128, sub.shape[1]).transpose(1, 0, 2))


def _prep_shared(inp):
    f32 = np.float32
    w_in = inp["w_in"][0]
    CW = 1024
    DN = 1024
    wA = np.stack([_blk(w_in, np.arange(s * CW + j * 128, s * CW + (j + 1) * 128))
                   for j in range(8) for s in (0, 2, 1)])
    base = 3 * CW
    wH = np.stack([_blk(w_in, np.arange(base + qi * DN + hh * 128, base + qi * DN + (hh + 1) * 128))
                   for hh in range(8) for qi in range(4)])
    wab = _blk(w_in, np.arange(base + 4 * DN, base + 4 * DN + 16))
    w_out = inp["w_out"][0]
    wO = np.stack([_blk(w_out, np.arange(fo * 128, (fo + 1) * 128)) for fo in range(16)])
    w_up = inp["w_up"][0]
    wU = np.stack([_blk(w_up, np.arange(half * DFF + j * 128, half * DFF + (j + 1) * 128))
                   for j in range(NHC) for half in range(2)])
    w_down = inp["w_down"][0]
    wD = np.stack([_blk(w_down[g * GRP * 128:(g + 1) * GRP * 128], np.arange(fo * 128, (fo + 1) * 128))
                   for g in range(NG) for fo in range(16)])
    wG = np.stack([_blk(inp["w_ple_gate"][0], np.arange(fo * 128, (fo + 1) * 128)) for fo in range(16)])
    wP = np.stack([_blk(inp["w_ple_proj"][0], np.arange(fo * 128, (fo + 1) * 128)) for fo in range(16)])

    par = np.zeros((128, NPAR), f32)

    def colblk(v):
        return v.reshape(-1, 128).T

    par[:, PO["g1"]:PO["g1"] + 16] = colblk(inp["norm_mix_g"][0])
    par[:, PO["g2"]:PO["g2"] + 16] = colblk(inp["norm_ffn_g"][0])
    par[:, PO["g3"]:PO["g3"] + 16] = colblk(inp["norm_ple_g"][0])
    par[:, PO["gf"]:PO["gf"] + 16] = colblk(inp["final_norm_g"])
    ca = inp["conv_a_w"][0]
    for j in range(8):
        par[:, PO["ca"] + 3 * j:PO["ca"] + 3 * j + 3] = ca[:, j * 128:(j + 1) * 128].T
    cq = inp["conv_qkv_w"][0]
    for b in range(24):
        par[:, PO["cq"] + 4 * b:PO["cq"] + 4 * b + 4] = cq[:, b * 128:(b + 1) * 128].T
    cf = inp["conv_ffn_w"][0]
    for j in range(NHC):
        for half in range(2):
            b = 2 * j + half
            c0 = half * DFF + j * 128
            par[:, PO["cf"] + 3 * b:PO["cf"] + 3 * b + 3] = cf[:, c0:c0 + 128].T
    par[:, PO["dng"]] = inp["dn_norm_g"][0]
    par[:, PO["dtb"]:PO["dtb"] + 8] = inp["dt_bias"][0][None, :]
    par[:, PO["alog"]:PO["alog"] + 8] = inp["a_log"][0][None, :]
    par[:, PO["ident"]:PO["ident"] + 128] = np.eye(128, dtype=f32)
    i = np.arange(64)
    par[0:64, PO["mus"]:PO["mus"] + 64] = (i[None, :] > i[:, None])
    par[0:64, PO["mui"]:PO["mui"] + 64] = (i[None, :] >= i[:, None])
    par[0:64, PO["mls"]:PO["mls"] + 64] = (i[None, :] < i[:, None])
    return dict(par=par, wA=wA, wH=wH, wab=wab, wO=wO, wU=wU, wD=wD, wG=wG, wP=wP)


def kernel(**inp):
    x = np.asarray(inp["x"], np.float32)
    p = np.asarray(inp["p"], np.float32)[0]
    inp = {k: np.asarray(v, np.float32) for k, v in inp.items()}
    shared = _prep_shared(inp)
    in_maps = []
    for c in range(8):
        b, half = c // 2, c % 2
        ntok = 2048 * (half + 1)
        xs = np.zeros((SEQ, D), np.float32)
        xs[SEQ - ntok:] = x[b, 0:ntok]
        xin = np.ascontiguousarray(xs.T.reshape(KC, 128, SEQ))
        pm = np.zeros((NMAIN, 256), np.float32)
        n_av = min(NMAIN, ntok)
        pm[NMAIN - n_av:] = p[b, ntok - n_av:ntok]
        pin = np.ascontiguousarray(pm.T.reshape(2, 128, NMAIN))
        m = dict(shared)
        m["xin"] = xin
        m["pin"] = pin
        in_maps.append(m)
    nc = build_nc()
    res = run_bass_kernel_spmd(nc, in_maps, core_ids=list(range(8)))
    outp = np.zeros((4, 4096, D), np.float32)
    for c in range(8):
        b, half = c // 2, c % 2
        o = res.results[c]["out"].reshape(D, NOUT).T
        outp[b, half * 2048:(half + 1) * 2048] = o
    return outp
```

```python
import numpy as np
from contextlib import ExitStack
import concourse.bass as bass
import concourse.mybir as mybir
from concourse.bass_utils import run_bass_kernel_spmd

F32 = mybir.dt.float32
BF16 = mybir.dt.bfloat16
AF = mybir.ActivationFunctionType
ALU = mybir.AluOpType

ENGS = ("pe", "act", "dve", "pool", "sp")
NDMASEM = 24


class Op:
    __slots__ = ("eng", "fn", "deps", "sig", "sigval", "idx", "dma", "dsem", "dval", "prev_on_sem")

    def __init__(self, eng, fn, dma):
        self.eng = eng
        self.fn = fn
        self.deps = []
        self.sig = False
        self.sigval = 0
        self.idx = 0
        self.dma = dma
        self.dsem = None
        self.dval = 0
        self.prev_on_sem = None


class Prog:
    def __init__(self, nc, es):
        self.nc = nc
        self.es = es
        self.streams = {e: [] for e in ENGS}
        self.last_w = {}
        self.readers = {}
        self.esem = {e: es.enter_context(nc.semaphore("s_" + e)) for e in ENGS if e != "sp"}
        self.dsems = {}
        self.dcount = {}
        self.dlast = {}
        self.dn = {e: 0 for e in ENGS}

    def _dma_sem(self, eng):
        i = self.dn[eng] % NDMASEM
        self.dn[eng] += 1
        key = (eng, i)
        if key not in self.dsems:
            self.dsems[key] = self.es.enter_context(self.nc.semaphore("d_%s_%d" % (eng, i)))
            self.dcount[key] = 0
            self.dlast[key] = None
        return key

    def op(self, eng, fn, r=(), w=(), dma=False):
        o = Op(eng, fn, dma)
        st = self.streams[eng]
        o.idx = len(st)
        best = {}

        def add(d, raw):
            if d is None:
                return
            if d.dma:
                best[("d", id(d))] = d
                return
            if d.eng == eng:
                if eng == "pe":
                    return
            k = d.eng
            if k not in best or best[k].idx < d.idx:
                best[k] = d

        for t in r:
            add(self.last_w.get(t), True)
        for t in w:
            add(self.last_w.get(t), False)
            for rd in self.readers.get(t, ()):
                add(rd, False)
        o.deps = list(best.values())
        for d in o.deps:
            if not d.dma:
                d.sig = True
        for t in r:
            self.readers.setdefault(t, []).append(o)
        for t in w:
            self.last_w[t] = o
            self.readers[t] = []
        if dma:
            key = self._dma_sem(eng)
            o.dsem = key
            o.prev_on_sem = self.dlast[key]
            self.dcount[key] += 16
            o.dval = self.dcount[key]
            self.dlast[key] = o
        st.append(o)
        return o

    def emit(self):
        nc = self.nc
        for e in ENGS:
            c = 0
            for o in self.streams[e]:
                if o.sig:
                    c += 1
                    o.sigval = c
        streams = self.streams
        esem = self.esem
        dsems = self.dsems

        def run(ename, eng):
            seen = {}
            for o in streams[ename]:
                waits = []
                for d in o.deps:
                    if d.dma:
                        waits.append((dsems[d.dsem], d.dval, d.dsem))
                    else:
                        waits.append((esem[d.eng], d.sigval, d.eng))
                if o.dma and o.prev_on_sem is not None:
                    p = o.prev_on_sem
                    waits.append((dsems[p.dsem], p.dval, p.dsem))
                for sem, val, key in waits:
                    if seen.get(key, 0) >= val:
                        continue
                    seen[key] = val
                    eng.wait_ge(sem, val)
                ins = o.fn(eng)
                if o.dma:
                    ins.then_inc(dsems[o.dsem], 16)
                elif o.sig:
                    ins.then_inc(esem[ename], 1)

        with nc.Block() as block:
            @block.tensor
            def _(eng):
                run("pe", eng)

            @block.scalar
            def _(eng):
                run("act", eng)

            @block.vector
            def _(eng):
                run("dve", eng)

            @block.gpsimd
            def _(eng):
                run("pool", eng)

            @block.sync
            def _(eng):
                run("sp", eng)


D = 2048
KC = 16
T = 704
NT = 352
NCH = 11
C = 64
NPASS_PRE = 3
NPASS_MAIN = 3
SEQ = T * (NPASS_PRE + NPASS_MAIN)
NMAIN = T * NPASS_MAIN
HALO = 64
NOUT = 2048
DFF = 5632
NHC = 44
GRP = 11
NG = 4
EPS = 1e-6
NSLOT = 6
NWK = 8
NCG = 6
GROUPS = ((0, 6), (6, 11))
HEADS0 = (0, 2, 4, 6)
HEADS1 = (1, 3, 5, 7)
STAGGER = 20

PO = {}
_o = 0
for _n, _w in (("g1", 16), ("g2", 16), ("g3", 16), ("gf", 16), ("ca", 24), ("cq", 96), ("cf", 264),
               ("dng", 1), ("dtb", 8), ("alog", 8), ("ident", 128), ("mus", 64), ("mui", 64), ("mls", 64)):
    PO[_n] = _o
    _o += _w
NPAR = _o


def build_nc(debug=False):
    nc = bass.Bass("TRN2", target_bir_lowering=False)
    dt = nc.dram_tensor
    xin = dt("xin", [KC, 128, SEQ], F32, kind="ExternalInput").ap()
    pin = dt("pin", [2, 128, NMAIN], F32, kind="ExternalInput").ap()
    par = dt("par", [128, NPAR], F32, kind="ExternalInput").ap()
    wA = dt("wA", [24, 128, KC, 128], F32, kind="ExternalInput").ap()
    wH = dt("wH", [32, 128, KC, 128], F32, kind="ExternalInput").ap()
    wab = dt("wab", [128, KC, 16], F32, kind="ExternalInput").ap()
    wO = dt("wO", [16, 128, KC, 128], F32, kind="ExternalInput").ap()
    wU = dt("wU", [88, 128, KC, 128], F32, kind="ExternalInput").ap()
    wD = dt("wD", [NG * 16, 128, GRP, 128], F32, kind="ExternalInput").ap()
    wG = dt("wG", [16, 128, KC, 128], F32, kind="ExternalInput").ap()
    wP = dt("wP", [16, 128, 2, 128], F32, kind="ExternalInput").ap()
    out = dt("out", [KC, 128, NOUT], F32, kind="ExternalOutput").ap()

    with ExitStack() as es:
        P = Prog(nc, es)
        sb = lambda name, shape, dtype: es.enter_context(nc.sbuf_tensor(name, shape, dtype))

        parS = sb("parS", [128, NPAR], F32)
        x = sb("x", [128, KC, T], F32)
        h = sb("h", [128, KC, T], BF16)
        y = sb("y", [128, KC, T], BF16)
        wslot = [sb("ws%d" % i, [128, KC, 128], BF16) for i in range(NSLOT)]
        wabS = sb("wabS", [128, KC, 16], BF16)
        wk = [sb("wk%d" % i, [128, T + 4], F32) for i in range(NWK)]
        sqb = [sb("sqb%d" % i, [128, T], BF16) for i in range(2)]
        identb = sb("identb", [128, 128], BF16)
        onesD = sb("onesD", [128, 128], BF16)
        onesH = sb("onesH", [128, 128], BF16)
        ones1 = sb("ones1", [128, 128], BF16)
        onesF = sb("onesF", [64, 128], F32)
        epsT = sb("epsT", [128, 1], F32)
        hista = sb("hista", [128, 8, 2], F32)
        histq = sb("histq", [128, 24, 3], F32)
        histf = sb("histf", [128, 88, 2], F32)
        pSb = sb("pSb", [128, 2, T], BF16)
        NG8 = NCH * 8
        gt = [sb("gt%d" % i, [64, NCH, 8], F32) for i in range(3)]
        gsb = sb("gsb", [64, NCH, 8], F32)
        gcS = sb("gcS", [64, NCH, 8], F32)
        betaS = sb("betaS", [64, NCH, 8], F32)
        eklS = sb("eklS", [64, NCH, 8], F32)
        nbkS = sb("nbkS", [64, NCH, 8], F32)
        eglS = sb("eglS", [128, NCH, 8], F32)
        negA = sb("negA", [64, 8], F32)
        S32 = sb("S32", [128, 8, 128], F32)
        Sbf = sb("Sbf", [128, 8, 128], BF16)

        class Z:
            pass
        GW = NCG * C
        SC = []
        for s_ in range(2):
            z_ = Z()
            n_ = lambda nm: "%s_%d" % (nm, s_)
            z_.qs = sb(n_("qs"), [128, T], BF16)
            z_.qd = sb(n_("qd"), [128, T], BF16)
            z_.kT = sb(n_("kT"), [128, T], BF16)
            z_.vT = sb(n_("vT"), [128, T], BF16)
            z_.zs = sb(n_("zs"), [128, T], BF16)
            z_.tm = [sb(n_("tm%d" % i), [128, GW], F32) for i in range(2)]
            z_.eg = sb(n_("eg"), [128, GW], BF16)
            z_.gcrow = sb(n_("gcrow"), [128, GW], F32)
            z_.E1u = sb(n_("E1u"), [64, NCG, C], BF16)
            z_.E1i = sb(n_("E1i"), [64, NCG, C], BF16)
            z_.E2l = sb(n_("E2l"), [64, NCG, C], BF16)
            z_.Xb = [sb(n_("Xb%d" % i), [64, NCG, C], BF16) for i in range(2)]
            z_.Yb = [sb(n_("Yb%d" % i), [64, NCG, C], BF16) for i in range(2)]
            z_.Qb = [sb(n_("Qb%d" % i), [64, NCG, C], BF16) for i in range(2)]
            z_.QKm = sb(n_("QKm"), [64, NCG, C], BF16)
            z_.kbg = sb(n_("kbg"), [64, NCG, 128], BF16)
            z_.kdec = sb(n_("kdec"), [64, NCG, 128], BF16)
            z_.vb = sb(n_("vb"), [64, NCG, 128], BF16)
            z_.wTn = sb(n_("wTn"), [128, GW], BF16)
            z_.vnew = [sb(n_("vnew%d" % i), [64, 128], BF16) for i in range(2)]
            SC.append(z_)
        ps = [es.enter_context(nc.psum_tensor("ps%d" % i, [128, 2, 512], F32)) for i in range(4)]
        psf = [p_[:].rearrange("p a b -> p (a b)") for p_ in ps]

        def PT(i):
            return [("P", i, 0), ("P", i, 1)]

        def act(out_, in_, func, r, w, bias=0.0, scale=1.0):
            return P.op("act", lambda e: e.activation(out=out_, in_=in_, func=func, bias=bias, scale=scale), r, w)

        def rsq(out_, in_, r, w):
            act(out_, in_, AF.Ln, list(r) + ["epsT"], w, bias=epsT[:, 0:1])
            act(out_, out_, AF.Exp, w, w, scale=-0.5)

        def tt(eng, out_, in0, in1, op, r, w):
            return P.op(eng, lambda e: e.tensor_tensor(out=out_, in0=in0, in1=in1, op=op), r, w)

        def ts(eng, out_, in0, s1, s2, op0, op1, r, w):
            return P.op(eng, lambda e: e.tensor_scalar(out=out_, in0=in0, scalar1=s1, scalar2=s2, op0=op0, op1=op1), r, w)

        def ts1(eng, out_, in0, s1, op0, r, w):
            return P.op(eng, lambda e: e.tensor_single_scalar(out=out_, in_=in0, scalar=s1, op=op0), r, w)

        def stt(eng, out_, in0, sc, in1, op0, op1, r, w):
            return P.op(eng, lambda e: e.scalar_tensor_tensor(out=out_, in0=in0, scalar=sc, in1=in1, op0=op0, op1=op1), r, w)

        def cp(eng, out_, in_, r, w):
            return P.op(eng, lambda e: e.tensor_copy(out=out_, in_=in_), r, w)

        def mm(out_, lhsT, rhs, start, stop, r, w):
            return P.op("pe", lambda e: e.matmul(out_, lhsT=lhsT, rhs=rhs, start=start, stop=stop), r, w)

        def dma(eng, out_, in_, r, w):
            return P.op(eng, lambda e: e.dma_start(out=out_, in_=in_), r, w, dma=True)

        def pcol(name, j=0, n=1):
            o = PO[name] + j
            return parS[:, o:o + n]

        wkfree = list(range(NWK))

        def getwk():
            assert wkfree, "wk pool exhausted"
            i = wkfree.pop(0)
            return wk[i], ("wk", i), i

        def freewk(i):
            assert i not in wkfree
            wkfree.append(i)

        sqn = [0]

        def getsq():
            i = sqn[0] % 2
            sqn[0] += 1
            return sqb[i], ("sq", i)

        bigp = [0]
        BIGB = [(0, 0), (0, 1), (1, 0), (1, 1), (3, 0), (3, 1)]

        def bigbank():
            i, j = BIGB[bigp[0] % len(BIGB)]
            bigp[0] += 1
            return ps[i][:, j, 0:NT], ("P", i, j)

        wsn = [0]

        def load_w(src, kc):
            s = wsn[0] % NSLOT
            wsn[0] += 1
            dma("pool", wslot[s][:, 0:kc, :], src, [], [("ws", s)])
            return wslot[s], ("ws", s)

        dma("sp", parS[:], par, [], ["par"])
        dma("pool", wabS[:], wab, [], ["wab"])
        cp("dve", identb[:], pcol("ident", 0, 128), ["par"], ["identb"])
        P.op("dve", lambda e: e.memset(onesD[:], 1.0 / D), [], ["onesD"])
        P.op("dve", lambda e: e.memset(onesH[:], 1.0 / 128), [], ["onesH"])
        P.op("dve", lambda e: e.memset(ones1[:], 1.0), [], ["ones1"])
        P.op("dve", lambda e: e.memset(onesF[:], 1.0), [], ["onesF"])
        P.op("dve", lambda e: e.memset(epsT[:], EPS), [], ["epsT"])
        P.op("pool", lambda e: e.memset(hista[:], 0.0), [], ["hista"])
        P.op("pool", lambda e: e.memset(histq[:], 0.0), [], [("histq", b_) for b_ in range(24)])
        P.op("pool", lambda e: e.memset(histf[:], 0.0), [], [("histf", b_) for b_ in range(88)])
        P.op("pool", lambda e: e.memset(S32[:], 0.0), [], [("S", hh) for hh in range(8)])
        P.op("pool", lambda e: e.memset(Sbf[:], 0.0), [], [("Sb", hh) for hh in range(8)])
        act(negA[:], parS[0:64, PO["alog"]:PO["alog"] + 8], AF.Exp, ["par"], ["negA0"])
        ts1("dve", negA[:], negA[:], -1.0, ALU.mult, ["negA0"], ["negA"])

        outtags = []

        def rmsnorm(gname, final_tok=None):
            st = ps[2]
            for fc in range(KC):
                sq, sqt = getsq()
                act(sq[:], x[:, fc, :], AF.Square, [("x", fc)], [sqt])
                for n in range(2):
                    mm(st[:, n, 0:NT], onesD[:], sq[:, n * NT:(n + 1) * NT], fc == 0, fc == KC - 1,
                       [sqt, "onesD"], [("P", 2, n)])
            rstd, rstt, rsti = getwk()
            for n in range(2):
                rsq(rstd[:, n * NT:(n + 1) * NT], st[:, n, 0:NT], [("P", 2, n)], [rstt])
            for fc in range(KC):
                if final_tok is None:
                    stt("dve", h[:, fc, :], x[:, fc, :], pcol(gname, fc), rstd[:, 0:T], ALU.mult, ALU.mult,
                        [("x", fc), rstt, "par"], [("h", fc)])
                else:
                    o0, skip = final_tok
                    wt, wtag, wi = getwk()
                    stt("dve", wt[:, 0:T], x[:, fc, :], pcol(gname, fc), rstd[:, 0:T], ALU.mult, ALU.mult,
                        [("x", fc), rstt, "par"], [wtag])
                    dma("sp", out[fc, :, o0:o0 + T - skip], wt[:, skip:T], [wtag], [("out", fc, o0)])
                    outtags.append(("out", fc, o0))
                    freewk(wi)
            freewk(rsti)

        def proj_to(wsrc, kc, act_src, act_tag, evac, banks=None):
            wt, wtag = load_w(wsrc, kc)
            pbs = []
            for n in range(2):
                if banks is None:
                    pbs.append(bigbank())
                else:
                    pbs.append((ps[banks][:, n, 0:NT], ("P", banks, n)))
            for k in range(kc):
                for n in range(2):
                    pb, ptag = pbs[n]
                    mm(pb, wt[:, k, :], act_src[:, k, n * NT:(n + 1) * NT], k == 0, k == kc - 1,
                       [wtag, (act_tag, k)], [ptag])
            for n in range(2):
                evac(n, pbs[n][0], pbs[n][1])

        def conv_stage(blk_hist, hist_tag, H, wname, wj, K):
            stg, stag, stgi = getwk()
            act(stg[:, 0:H], blk_hist, AF.Copy, [hist_tag], [stag])

            def finish():
                act(blk_hist, stg[:, T:T + H], AF.Copy, [stag], [hist_tag])
                c0, c0t, c0i = getwk()
                act(c0[:, 0:T], stg[:, H:H + T], AF.Copy, [stag, "par"], [c0t], scale=pcol(wname, wj + K - 1))
                for j in range(K - 2, -1, -1):
                    sh = H - (K - 1 - j)
                    stt("dve", c0[:, 0:T], stg[:, sh:sh + T], pcol(wname, wj + j), c0[:, 0:T], ALU.mult, ALU.add,
                        [stag, c0t, "par"], [c0t])
                freewk(stgi)
                return c0, c0t, c0i

            return stg, stag, finish

        def phase_A():
            for j in range(8):
                axs, axt, axi = getwk()

                def ev_ax(n, pb, ptag):
                    act(axs[:, n * NT:(n + 1) * NT], pb, AF.Copy, [ptag], [axt])
                proj_to(wA[3 * j + 0], KC, h, "h", ev_ax)
                stg, stag, finish = conv_stage(hista[:, j, :], "hista", 2, "ca", 3 * j, 3)

                def ev_ac(n, pb, ptag):
                    tt("dve", stg[:, 2 + n * NT:2 + (n + 1) * NT], pb, axs[:, n * NT:(n + 1) * NT], ALU.mult,
                       [ptag, axt], [stag])
                proj_to(wA[3 * j + 1], KC, h, "h", ev_ac)
                freewk(axi)
                cv, cvt, cvi = finish()

                def ev_ab(n, pb, ptag):
                    tt("dve", y[:, j, n * NT:(n + 1) * NT], pb, cv[:, n * NT:(n + 1) * NT], ALU.mult,
                       [ptag, cvt], [("y", j)])
                proj_to(wA[3 * j + 2], KC, h, "h", ev_ab)
                freewk(cvi)

        def gating():
            pg = psf[3]
            pgv = pg[0:64, 0:NCH * 16].rearrange("p (c s) -> p c s", s=16)
            for c in range(NCH):
                for k in range(KC):
                    mm(pgv[:, c, :], h[:, k, c * C:(c + 1) * C], wabS[:, k, :], k == 0, k == KC - 1,
                       [("h", k), "wab"], [("P", 3, 0)])
            dtb = parS[0:64, PO["dtb"]:PO["dtb"] + 8].unsqueeze(1).to_broadcast([64, NCH, 8])
            tt("dve", gt[0][:], pgv[:, :, 0:8], dtb, ALU.add, [("P", 3, 0), "par"], ["gt0"])
            act(gt[1][:], gt[0][:], AF.Exp, ["gt0"], ["gt1"])
            act(gt[2][:], gt[1][:], AF.Ln, ["gt1"], ["gt2"], bias=1.0)
            tt("dve", gsb[:], gt[2][:], negA[:].unsqueeze(1).to_broadcast([64, NCH, 8]), ALU.mult,
               ["gt2", "negA"], ["gsb"])
            act(betaS[:], pgv[:, :, 8:16], AF.Sigmoid, [("P", 3, 0)], ["beta"])
            gflat = gsb[:].rearrange("p c s -> p (c s)")
            pc = psf[3][0:64, 512:512 + NG8]
            pl = psf[2][:, 0:NG8]
            mm(pc, parS[0:64, PO["mui"]:PO["mui"] + 64], gflat, True, True, ["gsb", "par"], [("P", 3, 1)])
            mm(pl, onesF[:], gflat, True, True, ["gsb", "onesF"], [("P", 2, 0)])
            pc3 = pc.rearrange("p (c s) -> p c s", s=8)
            pl3 = pl.rearrange("p (c s) -> p c s", s=8)
            cp("dve", gcS[:], pc3, [("P", 3, 1)], ["gcS"])
            act(eglS[:], pl3, AF.Exp, [("P", 2, 0)], ["eglS"])
            tt("dve", gt[0][:], pl3[0:64], gcS[:], ALU.subtract, [("P", 2, 0), "gcS"], ["gt0"])
            act(eklS[:], gt[0][:], AF.Exp, ["gt0"], ["eklS"])
            act(gt[1][:], gcS[:], AF.Exp, ["gcS"], ["gt1"])
            stt("dve", nbkS[:], betaS[:], -1.0, gt[1][:], ALU.mult, ALU.mult, ["beta", "gt1"], ["nbkS"])

        def head(hh, main, s):
            zz = SC[s]
            iA, iB = 2 * s, 2 * s + 1
            PA, PB = ps[iA], ps[iB]
            PAf, PBf = psf[iA], psf[iB]
            A0, A1, B0, B1 = ("P", iA, 0), ("P", iA, 1), ("P", iB, 0), ("P", iB, 1)
            tg = lambda nm: (nm, s)
            order = (("q", 0), ("k", 1), ("v", 2)) if main else (("k", 1), ("v", 2))
            for nm, qi in order:
                blk = qi * 8 + hh
                stg, stag, finish = conv_stage(histq[:, blk, :], ("histq", blk), 3, "cq", 4 * blk, 4)

                def ev(n, pb, ptag, stg=stg, stag=stag):
                    act(stg[:, 3 + n * NT:3 + (n + 1) * NT], pb, AF.Copy, [ptag], [stag])
                proj_to(wH[4 * hh + qi], KC, h, "h", ev, banks=iA)
                yield
                cv, cvt, cvi = finish()
                if nm == "v":
                    act(zz.vT[:], cv[:, 0:T], AF.Silu, [cvt], [tg("vT")])
                    freewk(cvi)
                    continue
                act(cv[:, 0:T], cv[:, 0:T], AF.Silu, [cvt], [cvt])
                sq, sqt = getsq()
                act(sq[:], cv[:, 0:T], AF.Square, [cvt], [sqt])
                for n in range(2):
                    mm(PB[:, n, 0:NT], ones1[:], sq[:, n * NT:(n + 1) * NT], True, True, [sqt, "ones1"], [("P", iB, n)])
                yield
                rn, rnt, rni = getwk()
                for n in range(2):
                    rsq(rn[:, n * NT:(n + 1) * NT], PB[:, n, 0:NT], [("P", iB, n)], [rnt])
                if nm == "k":
                    tt("dve", zz.kT[:], cv[:, 0:T], rn[:, 0:T], ALU.mult, [cvt, rnt], [tg("kT")])
                else:
                    stt("dve", zz.qs[:], cv[:, 0:T], 128.0 ** -0.5, rn[:, 0:T], ALU.mult, ALU.mult,
                        [cvt, rnt], [tg("qs")])
                freewk(cvi)
                freewk(rni)
            if main:
                def ev_z(n, pb, ptag):
                    act(zz.zs[:, n * NT:(n + 1) * NT], pb, AF.Silu, [ptag], [tg("zs")])
                proj_to(wH[4 * hh + 3], KC, h, "h", ev_z, banks=iA)
                yield

            for (c0, c1) in GROUPS:
                n_ = c1 - c0
                W = n_ * C
                t0_, t1_ = c0 * C, c1 * C
                v3 = lambda ap: ap.rearrange("p (c j) -> p c j", j=C)
                id3 = parS[0:64, PO["ident"]:PO["ident"] + 64].unsqueeze(1).to_broadcast([64, n_, C])
                gch = gcS[:, c0:c1, hh:hh + 1].to_broadcast([64, n_, C])
                bth = betaS[:, c0:c1, hh:hh + 1].to_broadcast([64, n_, C])
                tmA = zz.tm[0][0:64, 0:W]
                tmB = zz.tm[1][0:64, 0:W]
                tt("dve", v3(tmA), id3, gch, ALU.mult, ["par", "gcS"], [tg("tmA")])
                tt("dve", v3(tmB), id3, bth, ALU.mult, ["par", "beta"], [tg("tmB")])
                mm(PBf[:, 0:W], onesF[:], tmA, True, True, [tg("tmA"), "onesF"], [B0])
                mm(PBf[0:64, 512:512 + W], onesF[:, 0:64], tmB, True, True, [tg("tmB"), "onesF"], [B1])
                yield
                act(zz.gcrow[:, 0:W], PBf[:, 0:W], AF.Copy, [B0], [tg("gcrow")])
                if main:
                    act(zz.eg[:, 0:W], PBf[:, 0:W], AF.Exp, [B0], [tg("eg")])
                    tt("dve", zz.qd[:, t0_:t1_], zz.qs[:, t0_:t1_], zz.eg[:, 0:W], ALU.mult,
                       [tg("qs"), tg("eg")], [tg("qd")])
                brow3 = v3(PBf[0:64, 512:512 + W])
                tm3 = v3(tmA)
                gr3 = v3(zz.gcrow[0:64, 0:W])
                tt("dve", tm3, gr3, gch, ALU.subtract, [tg("gcrow"), "gcS"], [tg("tmA")])
                m13 = v3(tmB)
                ts1("dve", m13, tm3, 0.0, ALU.min, [tg("tmA")], [tg("tmB")])
                act(m13, m13, AF.Exp, [tg("tmB")], [tg("tmB")])
                mus = parS[0:64, PO["mus"]:PO["mus"] + 64].unsqueeze(1).to_broadcast([64, n_, C])
                mui = parS[0:64, PO["mui"]:PO["mui"] + 64].unsqueeze(1).to_broadcast([64, n_, C])
                mls = parS[0:64, PO["mls"]:PO["mls"] + 64].unsqueeze(1).to_broadcast([64, n_, C])
                E1u = zz.E1u[:, 0:n_, :]
                E1i = zz.E1i[:, 0:n_, :]
                E2l = zz.E2l[:, 0:n_, :]
                tt("dve", E1u, m13, mus, ALU.mult, [tg("tmB"), "par"], [tg("E1u")])
                if main:
                    tt("dve", E1i, m13, mui, ALU.mult, [tg("tmB"), "par"], [tg("E1i")])
                ts("dve", tm3, tm3, -1.0, 0.0, ALU.mult, ALU.min, [tg("tmA")], [tg("tmA")])
                act(tm3, tm3, AF.Exp, [tg("tmA")], [tg("tmA")])
                tt("dve", E2l, tm3, mls, ALU.mult, [tg("tmA"), "par"], [tg("E2l")])
                pkk = v3(PAf[0:64, 0:W])
                for c in range(c0, c1):
                    kc_ = zz.kT[:, c * C:(c + 1) * C]
                    mm(pkk[:, c - c0, :], kc_, kc_, True, True, [tg("kT")], [A0])
                if main:
                    pqk = v3(PAf[0:64, 512:512 + W])
                    for c in range(c0, c1):
                        mm(pqk[:, c - c0, :], zz.kT[:, c * C:(c + 1) * C], zz.qs[:, c * C:(c + 1) * C], True, True,
                           [tg("kT"), tg("qs")], [A1])
                yield
                X = [xb[:, 0:n_, :] for xb in zz.Xb]
                Y = [yb[:, 0:n_, :] for yb in zz.Yb]
                Q = [qb[:, 0:n_, :] for qb in zz.Qb]
                Xt = [tg("X0"), tg("X1")]
                Yt = [tg("Y0"), tg("Y1")]
                Qt = [tg("Q0"), tg("Q1")]
                if main:
                    tt("dve", zz.QKm[:, 0:n_, :], pqk, E1i, ALU.mult, [A1, tg("E1i")], [tg("QKm")])
                tt("dve", m13, pkk, E1u, ALU.mult, [A0, tg("E1u")], [tg("tmB")])
                tt("dve", X[0], m13, brow3, ALU.mult, [tg("tmB"), B1], [Xt[0]])
                tt("dve", tm3, pkk, E2l, ALU.mult, [A0, tg("E2l")], [tg("tmA")])
                tt("dve", Y[0], tm3, bth, ALU.mult, [tg("tmA"), "beta"], [Yt[0]])
                tt("dve", Q[0], id3, X[0], ALU.subtract, ["par", Xt[0]], [Qt[0]])
                pY = v3(PBf[0:64, 0:W])
                pX = v3(PAf[0:64, 0:W])
                pQ = v3(PBf[0:64, 512:512 + W])
                for r_ in range(1, 7):
                    if r_ <= 5:
                        a, b = (r_ - 1) % 2, r_ % 2
                        for c in range(n_):
                            mm(pY[:, c, :], X[a][:, c, :], Y[a][:, c, :], True, True, [Xt[a], Yt[a]], [B0])
                        if r_ < 5:
                            for c in range(n_):
                                mm(pX[:, c, :], Y[a][:, c, :], X[a][:, c, :], True, True, [Xt[a], Yt[a]], [A0])
                    if r_ >= 2:
                        lv = r_ - 1
                        qa, qb = (lv - 1) % 2, lv % 2
                        for c in range(n_):
                            mm(pQ[:, c, :], Y[qb][:, c, :], Q[qa][:, c, :], True, True, [Yt[qb], Qt[qa]], [B1])
                    yield
                    if r_ >= 2:
                        tt("dve", Q[qb], pQ, Q[qa], ALU.add, [B1, Qt[qa]], [Qt[qb]])
                    if r_ <= 5:
                        cp("dve", Y[b], pY, [B0], [Yt[b]])
                        if r_ < 5:
                            act(X[b], pX, AF.Copy, [A0], [Xt[b]])
                TT_ = Q[1]
                TTt = Qt[1]
                pk = PAf[0:64, 0:n_ * 128]
                pv_ = PBf[0:64, 0:n_ * 128]
                for c in range(c0, c1):
                    cc = c - c0
                    mm(pk[:, cc * 128:(cc + 1) * 128], zz.kT[:, c * C:(c + 1) * C], identb[:], True, True,
                       [tg("kT"), "identb"], [A0, A1])
                    mm(pv_[:, cc * 128:(cc + 1) * 128], zz.vT[:, c * C:(c + 1) * C], identb[:], True, True,
                       [tg("vT"), "identb"], [B0, B1])
                yield
                pk3 = pk.rearrange("p (c d) -> p c d", d=128)
                pv3 = pv_.rearrange("p (c d) -> p c d", d=128)
                nb_ = nbkS[:, c0:c1, hh:hh + 1].to_broadcast([64, n_, 128])
                ek_ = eklS[:, c0:c1, hh:hh + 1].to_broadcast([64, n_, 128])
                bt_ = betaS[:, c0:c1, hh:hh + 1].to_broadcast([64, n_, 128])
                kbg = zz.kbg[:, 0:n_, :]
                kdec = zz.kdec[:, 0:n_, :]
                vb = zz.vb[:, 0:n_, :]
                tt("dve", kbg, pk3, nb_, ALU.mult, [A0, A1, "nbkS"], [tg("kbg")])
                tt("dve", kdec, pk3, ek_, ALU.mult, [A0, A1, "eklS"], [tg("kdec")])
                tt("dve", vb, pv3, bt_, ALU.mult, [B0, B1, "beta"], [tg("vb")])
                pw = PAf[:, 0:W]
                for cc in range(n_):
                    mm(pw[:, cc * C:(cc + 1) * C], kbg[:, cc, :], TT_[:, cc, :], True, True, [tg("kbg"), TTt], [A0])
                yield
                act(zz.wTn[:, 0:W], pw, AF.Copy, [A0], [tg("wTn")])
                po = PAf[:, 512:512 + W]
                for c in range(c0, c1):
                    cc = c - c0
                    pvn = PB[0:64, 0, (c % 2) * 128:(c % 2) * 128 + 128]
                    mm(pvn, TT_[:, cc, :], vb[:, cc, :], True, False, [TTt, tg("vb")], [B0])
                    mm(pvn, zz.wTn[:, cc * C:(cc + 1) * C], Sbf[:, hh, :], False, True, [tg("wTn"), ("Sb", hh)], [B0])
                    yield
                    vn = zz.vnew[c % 2]
                    vnt = tg("vnew%d" % (c % 2))
                    act(vn[:], pvn, AF.Copy, [B0], [vnt])
                    if main:
                        mm(po[:, cc * C:(cc + 1) * C], Sbf[:, hh, :], zz.qd[:, c * C:(c + 1) * C], True, False,
                           [("Sb", hh), tg("qd")], [A1])
                        mm(po[:, cc * C:(cc + 1) * C], vn[:], zz.QKm[:, cc, :], False, True, [vnt, tg("QKm")], [A1])
                    pSn = PB[:, 1, (c % 2) * 128:(c % 2) * 128 + 128]
                    mm(pSn, kdec[:, cc, :], vn[:], True, True, [tg("kdec"), vnt], [B1])
                    yield
                    stt("dve", Sbf[:, hh, :], S32[:, hh, :], eglS[:, c, hh:hh + 1], pSn, ALU.mult, ALU.add,
                        [("S", hh), "eglS", B1], [("Sb", hh)])
                    stt("dve", S32[:, hh, :], S32[:, hh, :], eglS[:, c, hh:hh + 1], pSn, ALU.mult, ALU.add,
                        [("S", hh), "eglS", B1], [("S", hh)])
                if not main:
                    continue
                sq, sqt = getsq()
                act(sq[:, 0:W], po, AF.Square, [A1], [sqt])
                mm(PAf[:, 0:W], onesH[:], sq[:, 0:W], True, True, [sqt, "onesH"], [A0])
                yield
                rn, rnt, rni = getwk()
                rsq(rn[:, 0:W], PAf[:, 0:W], [A0], [rnt])
                og, ogt, ogi = getwk()
                stt("dve", og[:, 0:W], po, pcol("dng"), rn[:, 0:W], ALU.mult, ALU.mult, [A1, rnt, "par"], [ogt])
                tt("dve", y[:, 8 + hh, t0_:t1_], og[:, 0:W], zz.zs[:, t0_:t1_], ALU.mult, [ogt, tg("zs")], [("y", 8 + hh)])
                freewk(rni)
                freewk(ogi)

        def run_heads(main):
            def chain(s, hs):
                for hh in hs:
                    yield from head(hh, main, s)
            gens = [chain(0, HEADS0), chain(1, HEADS1)]
            alive = [True, True]
            npre = 6 if main else 3
            for _ in range(npre):
                next(gens[0])
            gating()
            for _ in range(STAGGER - npre):
                try:
                    next(gens[0])
                except StopIteration:
                    alive[0] = False
                    break
            while any(alive):
                for i in range(2):
                    if alive[i]:
                        try:
                            next(gens[i])
                        except StopIteration:
                            alive[i] = False

        def add_x(fo):
            def ev(n, pb, ptag):
                tt("dve", x[:, fo, n * NT:(n + 1) * NT], x[:, fo, n * NT:(n + 1) * NT], pb, ALU.add,
                   [("x", fo), ptag], [("x", fo)])
            return ev

        def phase_out():
            for fo in range(KC):
                proj_to(wO[fo], KC, y, "y", add_x(fo))

        def phase_ffn():
            for g in range(NG):
                for jj in range(GRP):
                    j = g * GRP + jj
                    res = []
                    for half in range(2):
                        blk = 2 * j + half
                        stg, stag, finish = conv_stage(histf[:, blk, :], ("histf", blk), 2, "cf", 3 * blk, 3)

                        def ev(n, pb, ptag, stg=stg, stag=stag):
                            act(stg[:, 2 + n * NT:2 + (n + 1) * NT], pb, AF.Copy, [ptag], [stag])
                        proj_to(wU[blk], KC, h, "h", ev)
                        res.append(finish())
                    (gc_, gct, gci), (vc_, vct, vci) = res
                    act(gc_[:, 0:T], gc_[:, 0:T], AF.Silu, [gct], [gct])
                    tt("dve", y[:, jj, :], gc_[:, 0:T], vc_[:, 0:T], ALU.mult, [gct, vct], [("y", jj)])
                    freewk(gci)
                    freewk(vci)
                for fo in range(KC):
                    proj_to(wD[g * 16 + fo], GRP, y, "y", add_x(fo))

        def phase_ple(mp):
            for i in range(2):
                pt_, ptt, pti = getwk()
                dma("sp", pt_[:, 0:T], pin[i, :, mp * T:(mp + 1) * T], [], [ptt])
                cp("dve", pSb[:, i, :], pt_[:, 0:T], [ptt], [("p", i)])
                freewk(pti)
            for fo in range(KC):
                sg, sgt, sgi = getwk()

                def ev_g(n, pb, ptag):
                    act(sg[:, n * NT:(n + 1) * NT], pb, AF.Sigmoid, [ptag], [sgt])
                proj_to(wG[fo], KC, h, "h", ev_g)

                def ev_p(n, pb, ptag):
                    tt("dve", sg[:, n * NT:(n + 1) * NT], sg[:, n * NT:(n + 1) * NT], pb, ALU.mult, [sgt, ptag], [sgt])
                    tt("dve", x[:, fo, n * NT:(n + 1) * NT], x[:, fo, n * NT:(n + 1) * NT], sg[:, n * NT:(n + 1) * NT],
                       ALU.add, [("x", fo), sgt], [("x", fo)])
                proj_to(wP[fo], 2, pSb, "p", ev_p)
                freewk(sgi)

        for ps_i in range(NPASS_PRE + NPASS_MAIN):
            main = ps_i >= NPASS_PRE
            t0 = ps_i * T
            for fc in range(KC):
                dma("sp", x[:, fc, :], xin[fc, :, t0:t0 + T], [], [("x", fc)])
            rmsnorm("g1")
            if main:
                phase_A()
            run_heads(main)
            if not main:
                continue
            mp = ps_i - NPASS_PRE
            phase_out()
            rmsnorm("g2")
            phase_ffn()
            rmsnorm("g3")
            phase_ple(mp)
            if mp == 0:
                rmsnorm("gf", final_tok=(0, HALO))
            else:
                rmsnorm("gf", final_tok=(mp * T - HALO, 0))
        P.op("sp", lambda e: e.nop(), r=outtags)
        P.emit()
    return nc


def _blk(w, cols):
    K = w.shape[0]
    sub = w[:, cols]
    return np.ascontiguousarray(sub.reshape(K // 128, 128, sub.shape[1]).transpose(1, 0, 2))


def _prep_shared(inp):
    f32 = np.float32
    w_in = inp["w_in"][0]
    CW = 1024
    DN = 1024
    wA = np.stack([_blk(w_in, np.arange(s * CW + j * 128, s * CW + (j + 1) * 128))
                   for j in range(8) for s in (0, 2, 1)])
    base = 3 * CW
    wH = np.stack([_blk(w_in, np.arange(base + qi * DN + hh * 128, base + qi * DN + (hh + 1) * 128))
                   for hh in range(8) for qi in range(4)])
    wab = _blk(w_in, np.arange(base + 4 * DN, base + 4 * DN + 16))
    w_out = inp["w_out"][0]
    wO = np.stack([_blk(w_out, np.arange(fo * 128, (fo + 1) * 128)) for fo in range(16)])
    w_up = inp["w_up"][0]
    wU = np.stack([_blk(w_up, np.arange(half * DFF + j * 128, half * DFF + (j + 1) * 128))
                   for j in range(NHC) for half in range(2)])
    w_down = inp["w_down"][0]
    wD = np.stack([_blk(w_down[g * GRP * 128:(g + 1) * GRP * 128], np.arange(fo * 128, (fo + 1) * 128))
                   for g in range(NG) for fo in range(16)])
    wG = np.stack([_blk(inp["w_ple_gate"][0], np.arange(fo * 128, (fo + 1) * 128)) for fo in range(16)])
    wP = np.stack([_blk(inp["w_ple_proj"][0], np.arange(fo * 128, (fo + 1) * 128)) for fo in range(16)])

    par = np.zeros((128, NPAR), f32)

    def colblk(v):
        return v.reshape(-1, 128).T

    par[:, PO["g1"]:PO["g1"] + 16] = colblk(inp["norm_mix_g"][0])
    par[:, PO["g2"]:PO["g2"] + 16] = colblk(inp["norm_ffn_g"][0])
    par[:, PO["g3"]:PO["g3"] + 16] = colblk(inp["norm_ple_g"][0])
    par[:, PO["gf"]:PO["gf"] + 16] = colblk(inp["final_norm_g"])
    ca = inp["conv_a_w"][0]
    for j in range(8):
        par[:, PO["ca"] + 3 * j:PO["ca"] + 3 * j + 3] = ca[:, j * 128:(j + 1) * 128].T
    cq = inp["conv_qkv_w"][0]
    for b in range(24):
        par[:, PO["cq"] + 4 * b:PO["cq"] + 4 * b + 4] = cq[:, b * 128:(b + 1) * 128].T
    cf = inp["conv_ffn_w"][0]
    for j in range(NHC):
        for half in range(2):
            b = 2 * j + half
            c0 = half * DFF + j * 128
            par[:, PO["cf"] + 3 * b:PO["cf"] + 3 * b + 3] = cf[:, c0:c0 + 128].T
    par[:, PO["dng"]] = inp["dn_norm_g"][0]
    par[:, PO["dtb"]:PO["dtb"] + 8] = inp["dt_bias"][0][None, :]
    par[:, PO["alog"]:PO["alog"] + 8] = inp["a_log"][0][None, :]
    par[:, PO["ident"]:PO["ident"] + 128] = np.eye(128, dtype=f32)
    i = np.arange(64)
    par[0:64, PO["mus"]:PO["mus"] + 64] = (i[None, :] > i[:, None])
    par[0:64, PO["mui"]:PO["mui"] + 64] = (i[None, :] >= i[:, None])
    par[0:64, PO["mls"]:PO["mls"] + 64] = (i[None, :] < i[:, None])
    return dict(par=par, wA=wA, wH=wH, wab=wab, wO=wO, wU=wU, wD=wD, wG=wG, wP=wP)


def kernel(**inp):
    x = np.asarray(inp["x"], np.float32)
    p = np.asarray(inp["p"], np.float32)[0]
    inp = {k: np.asarray(v, np.float32) for k, v in inp.items()}
    shared = _prep_shared(inp)
    in_maps = []
    for c in range(8):
        b, half = c // 2, c % 2
        ntok = 2048 * (half + 1)
        xs = np.zeros((SEQ, D), np.float32)
        xs[SEQ - ntok:] = x[b, 0:ntok]
        xin = np.ascontiguousarray(xs.T.reshape(KC, 128, SEQ))
        pm = np.zeros((NMAIN, 256), np.float32)
        n_av = min(NMAIN, ntok)
        pm[NMAIN - n_av:] = p[b, ntok - n_av:ntok]
        pin = np.ascontiguousarray(pm.T.reshape(2, 128, NMAIN))
        m = dict(shared)
        m["xin"] = xin
        m["pin"] = pin
        in_maps.append(m)
    nc = build_nc()
    res = run_bass_kernel_spmd(nc, in_maps, core_ids=list(range(8)))
    outp = np.zeros((4, 4096, D), np.float32)
    for c in range(8):
        b, half = c // 2, c % 2
        o = res.results[c]["out"].reshape(D, NOUT).T
        outp[b, half * 2048:(half + 1) * 2048] = o
    return outp
```

```python
import numpy as np
from contextlib import ExitStack
import concourse.bass as bass
import concourse.mybir as mybir
from concourse.bass_utils import run_bass_kernel_spmd

F32 = mybir.dt.float32
BF16 = mybir.dt.bfloat16
AF = mybir.ActivationFunctionType
ALU = mybir.AluOpType

ENGS = ("pe", "act", "dve", "pool", "sp")
NDMASEM = 24


class Op:
    __slots__ = ("eng", "fn", "deps", "sig", "sigval", "idx", "dma", "dsem", "dval", "prev_on_sem")

    def __init__(self, eng, fn, dma):
        self.eng = eng
        self.fn = fn
        self.deps = []
        self.sig = False
        self.sigval = 0
        self.idx = 0
        self.dma = dma
        self.dsem = None
        self.dval = 0
        self.prev_on_sem = None


class Prog:
    def __init__(self, nc, es):
        self.nc = nc
        self.es = es
        self.streams = {e: [] for e in ENGS}
        self.last_w = {}
        self.readers = {}
        self.esem = {e: es.enter_context(nc.semaphore("s_" + e)) for e in ENGS if e != "sp"}
        self.dsems = {}
        self.dcount = {}
        self.dlast = {}
        self.dn = {e: 0 for e in ENGS}

    def _dma_sem(self, eng):
        i = self.dn[eng] % NDMASEM
        self.dn[eng] += 1
        key = (eng, i)
        if key not in self.dsems:
            self.dsems[key] = self.es.enter_context(self.nc.semaphore("d_%s_%d" % (eng, i)))
            self.dcount[key] = 0
            self.dlast[key] = None
        return key

    def op(self, eng, fn, r=(), w=(), dma=False):
        o = Op(eng, fn, dma)
        st = self.streams[eng]
        o.idx = len(st)
        best = {}

        def add(d, raw):
            if d is None:
                return
            if d.dma:
                best[("d", id(d))] = d
                return
            if d.eng == eng:
                if eng == "pe":
                    return
            k = d.eng
            if k not in best or best[k].idx < d.idx:
                best[k] = d

        for t in r:
            add(self.last_w.get(t), True)
        for t in w:
            add(self.last_w.get(t), False)
            for rd in self.readers.get(t, ()):
                add(rd, False)
        o.deps = list(best.values())
        for d in o.deps:
            if not d.dma:
                d.sig = True
        for t in r:
            self.readers.setdefault(t, []).append(o)
        for t in w:
            self.last_w[t] = o
            self.readers[t] = []
        if dma:
            key = self._dma_sem(eng)
            o.dsem = key
            o.prev_on_sem = self.dlast[key]
            self.dcount[key] += 16
            o.dval = self.dcount[key]
            self.dlast[key] = o
        st.append(o)
        return o

    def emit(self):
        nc = self.nc
        for e in ENGS:
            c = 0
            for o in self.streams[e]:
                if o.sig:
                    c += 1
                    o.sigval = c
        streams = self.streams
        esem = self.esem
        dsems = self.dsems

        def run(ename, eng):
            seen = {}
            for o in streams[ename]:
                waits = []
                for d in o.deps:
                    if d.dma:
                        waits.append((dsems[d.dsem], d.dval, d.dsem))
                    else:
                        waits.append((esem[d.eng], d.sigval, d.eng))
                if o.dma and o.prev_on_sem is not None:
                    p = o.prev_on_sem
                    waits.append((dsems[p.dsem], p.dval, p.dsem))
                for sem, val, key in waits:
                    if seen.get(key, 0) >= val:
                        continue
                    seen[key] = val
                    eng.wait_ge(sem, val)
                ins = o.fn(eng)
                if o.dma:
                    ins.then_inc(dsems[o.dsem], 16)
                elif o.sig:
                    ins.then_inc(esem[ename], 1)

        with nc.Block() as block:
            @block.tensor
            def _(eng):
                run("pe", eng)

            @block.scalar
            def _(eng):
                run("act", eng)

            @block.vector
            def _(eng):
                run("dve", eng)

            @block.gpsimd
            def _(eng):
                run("pool", eng)

            @block.sync
            def _(eng):
                run("sp", eng)


D = 2048
KC = 16
T = 704
NT = 352
NCH = 11
C = 64
NPASS_PRE = 3
NPASS_MAIN = 3
SEQ = T * (NPASS_PRE + NPASS_MAIN)
NMAIN = T * NPASS_MAIN
HALO = 64
NOUT = 2048
DFF = 5632
NHC = 44
GRP = 11
NG = 4
EPS = 1e-6
NSLOT = 6
NWK = 8
NCG = 6
GROUPS = ((0, 6), (6, 11))
HEADS0 = (0, 2, 4, 6)
HEADS1 = (1, 3, 5, 7)
STAGGER = 20

PO = {}
_o = 0
for _n, _w in (("g1", 16), ("g2", 16), ("g3", 16), ("gf", 16), ("ca", 24), ("cq", 96), ("cf", 264),
               ("dng", 1), ("dtb", 8), ("alog", 8), ("ident", 128), ("mus", 64), ("mui", 64), ("mls", 64)):
    PO[_n] = _o
    _o += _w
NPAR = _o


def build_nc(debug=False):
    nc = bass.Bass("TRN2", target_bir_lowering=False)
    dt = nc.dram_tensor
    xin = dt("xin", [KC, 128, SEQ], F32, kind="ExternalInput").ap()
    pin = dt("pin", [2, 128, NMAIN], F32, kind="ExternalInput").ap()
    par = dt("par", [128, NPAR], F32, kind="ExternalInput").ap()
    wA = dt("wA", [24, 128, KC, 128], F32, kind="ExternalInput").ap()
    wH = dt("wH", [32, 128, KC, 128], F32, kind="ExternalInput").ap()
    wab = dt("wab", [128, KC, 16], F32, kind="ExternalInput").ap()
    wO = dt("wO", [16, 128, KC, 128], F32, kind="ExternalInput").ap()
    wU = dt("wU", [88, 128, KC, 128], F32, kind="ExternalInput").ap()
    wD = dt("wD", [NG * 16, 128, GRP, 128], F32, kind="ExternalInput").ap()
    wG = dt("wG", [16, 128, KC, 128], F32, kind="ExternalInput").ap()
    wP = dt("wP", [16, 128, 2, 128], F32, kind="ExternalInput").ap()
    out = dt("out", [KC, 128, NOUT], F32, kind="ExternalOutput").ap()

    with ExitStack() as es:
        P = Prog(nc, es)
        sb = lambda name, shape, dtype: es.enter_context(nc.sbuf_tensor(name, shape, dtype))

        parS = sb("parS", [128, NPAR], F32)
        x = sb("x", [128, KC, T], F32)
        h = sb("h", [128, KC, T], BF16)
        y = sb("y", [128, KC, T], BF16)
        wslot = [sb("ws%d" % i, [128, KC, 128], BF16) for i in range(NSLOT)]
        wabS = sb("wabS", [128, KC, 16], BF16)
        wk = [sb("wk%d" % i, [128, T + 4], F32) for i in range(NWK)]
        sqb = [sb("sqb%d" % i, [128, T], BF16) for i in range(2)]
        identb = sb("identb", [128, 128], BF16)
        onesD = sb("onesD", [128, 128], BF16)
        onesH = sb("onesH", [128, 128], BF16)
        ones1 = sb("ones1", [128, 128], BF16)
        onesF = sb("onesF", [64, 128], F32)
        epsT = sb("epsT", [128, 1], F32)
        hista = sb("hista", [128, 8, 2], F32)
        histq = sb("histq", [128, 24, 3], F32)
        histf = sb("histf", [128, 88, 2], F32)
        pSb = sb("pSb", [128, 2, T], BF16)
        NG8 = NCH * 8
        gt = [sb("gt%d" % i, [64, NCH, 8], F32) for i in range(3)]
        gsb = sb("gsb", [64, NCH, 8], F32)
        gcS = sb("gcS", [64, NCH, 8], F32)
        betaS = sb("betaS", [64, NCH, 8], F32)
        eklS = sb("eklS", [64, NCH, 8], F32)
        nbkS = sb("nbkS", [64, NCH, 8], F32)
        eglS = sb("eglS", [128, NCH, 8], F32)
        negA = sb("negA", [64, 8], F32)
        S32 = sb("S32", [128, 8, 128], F32)
        Sbf = sb("Sbf", [128, 8, 128], BF16)

        class Z:
            pass
        GW = NCG * C
        SC = []
        for s_ in range(2):
            z_ = Z()
            n_ = lambda nm: "%s_%d" % (nm, s_)
            z_.qs = sb(n_("qs"), [128, T], BF16)
            z_.qd = sb(n_("qd"), [128, T], BF16)
            z_.kT = sb(n_("kT"), [128, T], BF16)
            z_.vT = sb(n_("vT"), [128, T], BF16)
            z_.zs = sb(n_("zs"), [128, T], BF16)
            z_.tm = [sb(n_("tm%d" % i), [128, GW], F32) for i in range(2)]
            z_.eg = sb(n_("eg"), [128, GW], BF16)
            z_.gcrow = sb(n_("gcrow"), [128, GW], F32)
            z_.E1u = sb(n_("E1u"), [64, NCG, C], BF16)
            z_.E1i = sb(n_("E1i"), [64, NCG, C], BF16)
            z_.E2l = sb(n_("E2l"), [64, NCG, C], BF16)
            z_.Xb = [sb(n_("Xb%d" % i), [64, NCG, C], BF16) for i in range(2)]
            z_.Yb = [sb(n_("Yb%d" % i), [64, NCG, C], BF16) for i in range(2)]
            z_.Qb = [sb(n_("Qb%d" % i), [64, NCG, C], BF16) for i in range(2)]
            z_.QKm = sb(n_("QKm"), [64, NCG, C], BF16)
            z_.kbg = sb(n_("kbg"), [64, NCG, 128], BF16)
            z_.kdec = sb(n_("kdec"), [64, NCG, 128], BF16)
            z_.vb = sb(n_("vb"), [64, NCG, 128], BF16)
            z_.wTn = sb(n_("wTn"), [128, GW], BF16)
            z_.vnew = [sb(n_("vnew%d" % i), [64, 128], BF16) for i in range(2)]
            SC.append(z_)
        ps = [es.enter_context(nc.psum_tensor("ps%d" % i, [128, 2, 512], F32)) for i in range(4)]
        psf = [p_[:].rearrange("p a b -> p (a b)") for p_ in ps]

        def PT(i):
            return [("P", i, 0), ("P", i, 1)]

        def act(out_, in_, func, r, w, bias=0.0, scale=1.0):
            return P.op("act", lambda e: e.activation(out=out_, in_=in_, func=func, bias=bias, scale=scale), r, w)

        def rsq(out_, in_, r, w):
            act(out_, in_, AF.Ln, list(r) + ["epsT"], w, bias=epsT[:, 0:1])
            act(out_, out_, AF.Exp, w, w, scale=-0.5)

        def tt(eng, out_, in0, in1, op, r, w):
            return P.op(eng, lambda e: e.tensor_tensor(out=out_, in0=in0, in1=in1, op=op), r, w)

        def ts(eng, out_, in0, s1, s2, op0, op1, r, w):
            return P.op(eng, lambda e: e.tensor_scalar(out=out_, in0=in0, scalar1=s1, scalar2=s2, op0=op0, op1=op1), r, w)

        def ts1(eng, out_, in0, s1, op0, r, w):
            return P.op(eng, lambda e: e.tensor_single_scalar(out=out_, in_=in0, scalar=s1, op=op0), r, w)

        def stt(eng, out_, in0, sc, in1, op0, op1, r, w):
            return P.op(eng, lambda e: e.scalar_tensor_tensor(out=out_, in0=in0, scalar=sc, in1=in1, op0=op0, op1=op1), r, w)

        def cp(eng, out_, in_, r, w):
            return P.op(eng, lambda e: e.tensor_copy(out=out_, in_=in_), r, w)

        def mm(out_, lhsT, rhs, start, stop, r, w):
            return P.op("pe", lambda e: e.matmul(out_, lhsT=lhsT, rhs=rhs, start=start, stop=stop), r, w)

        def dma(eng, out_, in_, r, w):
            return P.op(eng, lambda e: e.dma_start(out=out_, in_=in_), r, w, dma=True)

        def pcol(name, j=0, n=1):
            o = PO[name] + j
            return parS[:, o:o + n]

        wkfree = list(range(NWK))

        def getwk():
            assert wkfree, "wk pool exhausted"
            i = wkfree.pop(0)
            return wk[i], ("wk", i), i

        def freewk(i):
            assert i not in wkfree
            wkfree.append(i)

        sqn = [0]

        def getsq():
            i = sqn[0] % 2
            sqn[0] += 1
            return sqb[i], ("sq", i)

        bigp = [0]
        BIGB = [(0, 0), (0, 1), (1, 0), (1, 1), (3, 0), (3, 1)]

        def bigbank():
            i, j = BIGB[bigp[0] % len(BIGB)]
            bigp[0] += 1
            return ps[i][:, j, 0:NT], ("P", i, j)

        wsn = [0]

        def load_w(src, kc):
            s = wsn[0] % NSLOT
            wsn[0] += 1
            dma("pool", wslot[s][:, 0:kc, :], src, [], [("ws", s)])
            return wslot[s], ("ws", s)

        dma("sp", parS[:], par, [], ["par"])
        dma("pool", wabS[:], wab, [], ["wab"])
        cp("dve", identb[:], pcol("ident", 0, 128), ["par"], ["identb"])
        P.op("dve", lambda e: e.memset(onesD[:], 1.0 / D), [], ["onesD"])
        P.op("dve", lambda e: e.memset(onesH[:], 1.0 / 128), [], ["onesH"])
        P.op("dve", lambda e: e.memset(ones1[:], 1.0), [], ["ones1"])
        P.op("dve", lambda e: e.memset(onesF[:], 1.0), [], ["onesF"])
        P.op("dve", lambda e: e.memset(epsT[:], EPS), [], ["epsT"])
        P.op("pool", lambda e: e.memset(hista[:], 0.0), [], ["hista"])
        P.op("pool", lambda e: e.memset(histq[:], 0.0), [], [("histq", b_) for b_ in range(24)])
        P.op("pool", lambda e: e.memset(histf[:], 0.0), [], [("histf", b_) for b_ in range(88)])
        P.op("pool", lambda e: e.memset(S32[:], 0.0), [], [("S", hh) for hh in range(8)])
        P.op("pool", lambda e: e.memset(Sbf[:], 0.0), [], [("Sb", hh) for hh in range(8)])
        act(negA[:], parS[0:64, PO["alog"]:PO["alog"] + 8], AF.Exp, ["par"], ["negA0"])
        ts1("dve", negA[:], negA[:], -1.0, ALU.mult, ["negA0"], ["negA"])

        outtags = []

        def rmsnorm(gname, final_tok=None):
            st = ps[2]
            for fc in range(KC):
                sq, sqt = getsq()
                act(sq[:], x[:, fc, :], AF.Square, [("x", fc)], [sqt])
                for n in range(2):
                    mm(st[:, n, 0:NT], onesD[:], sq[:, n * NT:(n + 1) * NT], fc == 0, fc == KC - 1,
                       [sqt, "onesD"], [("P", 2, n)])
            rstd, rstt, rsti = getwk()
            for n in range(2):
                rsq(rstd[:, n * NT:(n + 1) * NT], st[:, n, 0:NT], [("P", 2, n)], [rstt])
            for fc in range(KC):
                if final_tok is None:
                    stt("dve", h[:, fc, :], x[:, fc, :], pcol(gname, fc), rstd[:, 0:T], ALU.mult, ALU.mult,
                        [("x", fc), rstt, "par"], [("h", fc)])
                else:
                    o0, skip = final_tok
                    wt, wtag, wi = getwk()
                    stt("dve", wt[:, 0:T], x[:, fc, :], pcol(gname, fc), rstd[:, 0:T], ALU.mult, ALU.mult,
                        [("x", fc), rstt, "par"], [wtag])
                    dma("sp", out[fc, :, o0:o0 + T - skip], wt[:, skip:T], [wtag], [("out", fc, o0)])
                    outtags.append(("out", fc, o0))
                    freewk(wi)
            freewk(rsti)

        def proj_to(wsrc, kc, act_src, act_tag, evac, banks=None):
            wt, wtag = load_w(wsrc, kc)
            pbs = []
            for n in range(2):
                if banks is None:
                    pbs.append(bigbank())
                else:
                    pbs.append((ps[banks][:, n, 0:NT], ("P", banks, n)))
            for k in range(kc):
                for n in range(2):
                    pb, ptag = pbs[n]
                    mm(pb, wt[:, k, :], act_src[:, k, n * NT:(n + 1) * NT], k == 0, k == kc - 1,
                       [wtag, (act_tag, k)], [ptag])
            for n in range(2):
                evac(n, pbs[n][0], pbs[n][1])

        def conv_stage(blk_hist, hist_tag, H, wname, wj, K):
            stg, stag, stgi = getwk()
            act(stg[:, 0:H], blk_hist, AF.Copy, [hist_tag], [stag])

            def finish():
                act(blk_hist, stg[:, T:T + H], AF.Copy, [stag], [hist_tag])
                c0, c0t, c0i = getwk()
                act(c0[:, 0:T], stg[:, H:H + T], AF.Copy, [stag, "par"], [c0t], scale=pcol(wname, wj + K - 1))
                for j in range(K - 2, -1, -1):
                    sh = H - (K - 1 - j)
                    stt("dve", c0[:, 0:T], stg[:, sh:sh + T], pcol(wname, wj + j), c0[:, 0:T], ALU.mult, ALU.add,
                        [stag, c0t, "par"], [c0t])
                freewk(stgi)
                return c0, c0t, c0i

            return stg, stag, finish

        def phase_A():
            for j in range(8):
                axs, axt, axi = getwk()

                def ev_ax(n, pb, ptag):
                    act(axs[:, n * NT:(n + 1) * NT], pb, AF.Copy, [ptag], [axt])
                proj_to(wA[3 * j + 0], KC, h, "h", ev_ax)
                stg, stag, finish = conv_stage(hista[:, j, :], "hista", 2, "ca", 3 * j, 3)

                def ev_ac(n, pb, ptag):
                    tt("dve", stg[:, 2 + n * NT:2 + (n + 1) * NT], pb, axs[:, n * NT:(n + 1) * NT], ALU.mult,
                       [ptag, axt], [stag])
                proj_to(wA[3 * j + 1], KC, h, "h", ev_ac)
                freewk(axi)
                cv, cvt, cvi = finish()

                def ev_ab(n, pb, ptag):
                    tt("dve", y[:, j, n * NT:(n + 1) * NT], pb, cv[:, n * NT:(n + 1) * NT], ALU.mult,
                       [ptag, cvt], [("y", j)])
                proj_to(wA[3 * j + 2], KC, h, "h", ev_ab)
                freewk(cvi)

        def gating():
            pg = psf[3]
            pgv = pg[0:64, 0:NCH * 16].rearrange("p (c s) -> p c s", s=16)
            for c in range(NCH):
                for k in range(KC):
                    mm(pgv[:, c, :], h[:, k, c * C:(c + 1) * C], wabS[:, k, :], k == 0, k == KC - 1,
                       [("h", k), "wab"], [("P", 3, 0)])
            dtb = parS[0:64, PO["dtb"]:PO["dtb"] + 8].unsqueeze(1).to_broadcast([64, NCH, 8])
            tt("dve", gt[0][:], pgv[:, :, 0:8], dtb, ALU.add, [("P", 3, 0), "par"], ["gt0"])
            act(gt[1][:], gt[0][:], AF.Exp, ["gt0"], ["gt1"])
            act(gt[2][:], gt[1][:], AF.Ln, ["gt1"], ["gt2"], bias=1.0)
            tt("dve", gsb[:], gt[2][:], negA[:].unsqueeze(1).to_broadcast([64, NCH, 8]), ALU.mult,
               ["gt2", "negA"], ["gsb"])
            act(betaS[:], pgv[:, :, 8:16], AF.Sigmoid, [("P", 3, 0)], ["beta"])
            gflat = gsb[:].rearrange("p c s -> p (c s)")
            pc = psf[3][0:64, 512:512 + NG8]
            pl = psf[2][:, 0:NG8]
            mm(pc, parS[0:64, PO["mui"]:PO["mui"] + 64], gflat, True, True, ["gsb", "par"], [("P", 3, 1)])
            mm(pl, onesF[:], gflat, True, True, ["gsb", "onesF"], [("P", 2, 0)])
            pc3 = pc.rearrange("p (c s) -> p c s", s=8)
            pl3 = pl.rearrange("p (c s) -> p c s", s=8)
            cp("dve", gcS[:], pc3, [("P", 3, 1)], ["gcS"])
            act(eglS[:], pl3, AF.Exp, [("P", 2, 0)], ["eglS"])
            tt("dve", gt[0][:], pl3[0:64], gcS[:], ALU.subtract, [("P", 2, 0), "gcS"], ["gt0"])
            act(eklS[:], gt[0][:], AF.Exp, ["gt0"], ["eklS"])
            act(gt[1][:], gcS[:], AF.Exp, ["gcS"], ["gt1"])
            stt("dve", nbkS[:], betaS[:], -1.0, gt[1][:], ALU.mult, ALU.mult, ["beta", "gt1"], ["nbkS"])

        def head(hh, main, s):
            zz = SC[s]
            iA, iB = 2 * s, 2 * s + 1
            PA, PB = ps[iA], ps[iB]
            PAf, PBf = psf[iA], psf[iB]
            A0, A1, B0, B1 = ("P", iA, 0), ("P", iA, 1), ("P", iB, 0), ("P", iB, 1)
            tg = lambda nm: (nm, s)
            order = (("q", 0), ("k", 1), ("v", 2)) if main else (("k", 1), ("v", 2))
            for nm, qi in order:
                blk = qi * 8 + hh
                stg, stag, finish = conv_stage(histq[:, blk, :], ("histq", blk), 3, "cq", 4 * blk, 4)

                def ev(n, pb, ptag, stg=stg, stag=stag):
                    act(stg[:, 3 + n * NT:3 + (n + 1) * NT], pb, AF.Copy, [ptag], [stag])
                proj_to(wH[4 * hh + qi], KC, h, "h", ev, banks=iA)
                yield
                cv, cvt, cvi = finish()
                if nm == "v":
                    act(zz.vT[:], cv[:, 0:T], AF.Silu, [cvt], [tg("vT")])
                    freewk(cvi)
                    continue
                act(cv[:, 0:T], cv[:, 0:T], AF.Silu, [cvt], [cvt])
                sq, sqt = getsq()
                act(sq[:], cv[:, 0:T], AF.Square, [cvt], [sqt])
                for n in range(2):
                    mm(PB[:, n, 0:NT], ones1[:], sq[:, n * NT:(n + 1) * NT], True, True, [sqt, "ones1"], [("P", iB, n)])
                yield
                rn, rnt, rni = getwk()
                for n in range(2):
                    rsq(rn[:, n * NT:(n + 1) * NT], PB[:, n, 0:NT], [("P", iB, n)], [rnt])
                if nm == "k":
                    tt("dve", zz.kT[:], cv[:, 0:T], rn[:, 0:T], ALU.mult, [cvt, rnt], [tg("kT")])
                else:
                    stt("dve", zz.qs[:], cv[:, 0:T], 128.0 ** -0.5, rn[:, 0:T], ALU.mult, ALU.mult,
                        [cvt, rnt], [tg("qs")])
                freewk(cvi)
                freewk(rni)
            if main:
                def ev_z(n, pb, ptag):
                    act(zz.zs[:, n * NT:(n + 1) * NT], pb, AF.Silu, [ptag], [tg("zs")])
                proj_to(wH[4 * hh + 3], KC, h, "h", ev_z, banks=iA)
                yield

            for (c0, c1) in GROUPS:
                n_ = c1 - c0
                W = n_ * C
                t0_, t1_ = c0 * C, c1 * C
                v3 = lambda ap: ap.rearrange("p (c j) -> p c j", j=C)
                id3 = parS[0:64, PO["ident"]:PO["ident"] + 64].unsqueeze(1).to_broadcast([64, n_, C])
                gch = gcS[:, c0:c1, hh:hh + 1].to_broadcast([64, n_, C])
                bth = betaS[:, c0:c1, hh:hh + 1].to_broadcast([64, n_, C])
                tmA = zz.tm[0][0:64, 0:W]
                tmB = zz.tm[1][0:64, 0:W]
                tt("dve", v3(tmA), id3, gch, ALU.mult, ["par", "gcS"], [tg("tmA")])
                tt("dve", v3(tmB), id3, bth, ALU.mult, ["par", "beta"], [tg("tmB")])
                mm(PBf[:, 0:W], onesF[:], tmA, True, True, [tg("tmA"), "onesF"], [B0])
                mm(PBf[0:64, 512:512 + W], onesF[:, 0:64], tmB, True, True, [tg("tmB"), "onesF"], [B1])
                yield
                act(zz.gcrow[:, 0:W], PBf[:, 0:W], AF.Copy, [B0], [tg("gcrow")])
                if main:
                    act(zz.eg[:, 0:W], PBf[:, 0:W], AF.Exp, [B0], [tg("eg")])
                    tt("dve", zz.qd[:, t0_:t1_], zz.qs[:, t0_:t1_], zz.eg[:, 0:W], ALU.mult,
                       [tg("qs"), tg("eg")], [tg("qd")])
                brow3 = v3(PBf[0:64, 512:512 + W])
                tm3 = v3(tmA)
                gr3 = v3(zz.gcrow[0:64, 0:W])
                tt("dve", tm3, gr3, gch, ALU.subtract, [tg("gcrow"), "gcS"], [tg("tmA")])
                m13 = v3(tmB)
                ts1("dve", m13, tm3, 0.0, ALU.min, [tg("tmA")], [tg("tmB")])
                ts("dve", tm3, tm3, -1.0, 0.0, ALU.mult, ALU.min, [tg("tmA")], [tg("tmA")])
                act(m13, m13, AF.Exp, [tg("tmB")], [tg("tmB")])
                act(tm3, tm3, AF.Exp, [tg("tmA")], [tg("tmA")])
                mus = parS[0:64, PO["mus"]:PO["mus"] + 64].unsqueeze(1).to_broadcast([64, n_, C])
                mui = parS[0:64, PO["mui"]:PO["mui"] + 64].unsqueeze(1).to_broadcast([64, n_, C])
                mls = parS[0:64, PO["mls"]:PO["mls"] + 64].unsqueeze(1).to_broadcast([64, n_, C])
                E1u = zz.E1u[:, 0:n_, :]
                E1i = zz.E1i[:, 0:n_, :]
                E2l = zz.E2l[:, 0:n_, :]
                tt("dve", E1u, m13, mus, ALU.mult, [tg("tmB"), "par"], [tg("E1u")])
                if main:
                    tt("dve", E1i, m13, mui, ALU.mult, [tg("tmB"), "par"], [tg("E1i")])
                tt("dve", E2l, tm3, mls, ALU.mult, [tg("tmA"), "par"], [tg("E2l")])
                pkk = v3(PAf[0:64, 0:W])
                for c in range(c0, c1):
                    kc_ = zz.kT[:, c * C:(c + 1) * C]
                    mm(pkk[:, c - c0, :], kc_, kc_, True, True, [tg("kT")], [A0])
                if main:
                    pqk = v3(PAf[0:64, 512:512 + W])
                    for c in range(c0, c1):
                        mm(pqk[:, c - c0, :], zz.kT[:, c * C:(c + 1) * C], zz.qs[:, c * C:(c + 1) * C], True, True,
                           [tg("kT"), tg("qs")], [A1])
                yield
                X = [xb[:, 0:n_, :] for xb in zz.Xb]
                Y = [yb[:, 0:n_, :] for yb in zz.Yb]
                Q = [qb[:, 0:n_, :] for qb in zz.Qb]
                Xt = [tg("X0"), tg("X1")]
                Yt = [tg("Y0"), tg("Y1")]
                Qt = [tg("Q0"), tg("Q1")]
                if main:
                    tt("dve", zz.QKm[:, 0:n_, :], pqk, E1i, ALU.mult, [A1, tg("E1i")], [tg("QKm")])
                tt("dve", m13, pkk, E1u, ALU.mult, [A0, tg("E1u")], [tg("tmB")])
                tt("dve", X[0], m13, brow3, ALU.mult, [tg("tmB"), B1], [Xt[0]])
                tt("dve", tm3, pkk, E2l, ALU.mult, [A0, tg("E2l")], [tg("tmA")])
                tt("dve", Y[0], tm3, bth, ALU.mult, [tg("tmA"), "beta"], [Yt[0]])
                tt("dve", Q[0], id3, X[0], ALU.subtract, ["par", Xt[0]], [Qt[0]])
                pY = v3(PBf[0:64, 0:W])
                pX = v3(PAf[0:64, 0:W])
                pQ = v3(PBf[0:64, 512:512 + W])
                for r_ in range(1, 7):
                    if r_ <= 5:
                        a, b = (r_ - 1) % 2, r_ % 2
                        for c in range(n_):
                            mm(pY[:, c, :], X[a][:, c, :], Y[a][:, c, :], True, True, [Xt[a], Yt[a]], [B0])
                        if r_ < 5:
                            for c in range(n_):
                                mm(pX[:, c, :], Y[a][:, c, :], X[a][:, c, :], True, True, [Xt[a], Yt[a]], [A0])
                    if r_ >= 2:
                        lv = r_ - 1
                        qa, qb = (lv - 1) % 2, lv % 2
                        for c in range(n_):
                            mm(pQ[:, c, :], Y[qb][:, c, :], Q[qa][:, c, :], True, True, [Yt[qb], Qt[qa]], [B1])
                    yield
                    if r_ >= 2:
                        tt("dve", Q[qb], pQ, Q[qa], ALU.add, [B1, Qt[qa]], [Qt[qb]])
                    if r_ <= 5:
                        cp("dve", Y[b], pY, [B0], [Yt[b]])
                        if r_ < 5:
                            act(X[b], pX, AF.Copy, [A0], [Xt[b]])
                TT_ = Q[1]
                TTt = Qt[1]
                pk = PAf[0:64, 0:n_ * 128]
                pv_ = PBf[0:64, 0:n_ * 128]
                for c in range(c0, c1):
                    cc = c - c0
                    mm(pk[:, cc * 128:(cc + 1) * 128], zz.kT[:, c * C:(c + 1) * C], identb[:], True, True,
                       [tg("kT"), "identb"], [A0, A1])
                    mm(pv_[:, cc * 128:(cc + 1) * 128], zz.vT[:, c * C:(c + 1) * C], identb[:], True, True,
                       [tg("vT"), "identb"], [B0, B1])
                yield
                pk3 = pk.rearrange("p (c d) -> p c d", d=128)
                pv3 = pv_.rearrange("p (c d) -> p c d", d=128)
                nb_ = nbkS[:, c0:c1, hh:hh + 1].to_broadcast([64, n_, 128])
                ek_ = eklS[:, c0:c1, hh:hh + 1].to_broadcast([64, n_, 128])
                bt_ = betaS[:, c0:c1, hh:hh + 1].to_broadcast([64, n_, 128])
                kbg = zz.kbg[:, 0:n_, :]
                kdec = zz.kdec[:, 0:n_, :]
                vb = zz.vb[:, 0:n_, :]
                tt("dve", kbg, pk3, nb_, ALU.mult, [A0, A1, "nbkS"], [tg("kbg")])
                tt("dve", kdec, pk3, ek_, ALU.mult, [A0, A1, "eklS"], [tg("kdec")])
                tt("dve", vb, pv3, bt_, ALU.mult, [B0, B1, "beta"], [tg("vb")])
                pw = PAf[:, 0:W]
                for cc in range(n_):
                    mm(pw[:, cc * C:(cc + 1) * C], kbg[:, cc, :], TT_[:, cc, :], True, True, [tg("kbg"), TTt], [A0])
                yield
                act(zz.wTn[:, 0:W], pw, AF.Copy, [A0], [tg("wTn")])
                po = PAf[:, 512:512 + W]
                for c in range(c0, c1):
                    cc = c - c0
                    pvn = PB[0:64, 0, (c % 2) * 128:(c % 2) * 128 + 128]
                    mm(pvn, TT_[:, cc, :], vb[:, cc, :], True, False, [TTt, tg("vb")], [B0])
                    mm(pvn, zz.wTn[:, cc * C:(cc + 1) * C], Sbf[:, hh, :], False, True, [tg("wTn"), ("Sb", hh)], [B0])
                    yield
                    vn = zz.vnew[c % 2]
                    vnt = tg("vnew%d" % (c % 2))
                    act(vn[:], pvn, AF.Copy, [B0], [vnt])
                    if main:
                        mm(po[:, cc * C:(cc + 1) * C], Sbf[:, hh, :], zz.qd[:, c * C:(c + 1) * C], True, False,
                           [("Sb", hh), tg("qd")], [A1])
                        mm(po[:, cc * C:(cc + 1) * C], vn[:], zz.QKm[:, cc, :], False, True, [vnt, tg("QKm")], [A1])
                    pSn = PB[:, 1, (c % 2) * 128:(c % 2) * 128 + 128]
                    mm(pSn, kdec[:, cc, :], vn[:], True, True, [tg("kdec"), vnt], [B1])
                    yield
                    stt("dve", Sbf[:, hh, :], S32[:, hh, :], eglS[:, c, hh:hh + 1], pSn, ALU.mult, ALU.add,
                        [("S", hh), "eglS", B1], [("Sb", hh)])
                    stt("dve", S32[:, hh, :], S32[:, hh, :], eglS[:, c, hh:hh + 1], pSn, ALU.mult, ALU.add,
                        [("S", hh), "eglS", B1], [("S", hh)])
                if not main:
                    continue
                sq, sqt = getsq()
                act(sq[:, 0:W], po, AF.Square, [A1], [sqt])
                mm(PAf[:, 0:W], onesH[:], sq[:, 0:W], True, True, [sqt, "onesH"], [A0])
                yield
                rn, rnt, rni = getwk()
                rsq(rn[:, 0:W], PAf[:, 0:W], [A0], [rnt])
                og, ogt, ogi = getwk()
                stt("dve", og[:, 0:W], po, pcol("dng"), rn[:, 0:W], ALU.mult, ALU.mult, [A1, rnt, "par"], [ogt])
                tt("dve", y[:, 8 + hh, t0_:t1_], og[:, 0:W], zz.zs[:, t0_:t1_], ALU.mult, [ogt, tg("zs")], [("y", 8 + hh)])
                freewk(rni)
                freewk(ogi)

        def run_heads(main):
            def chain(s, hs):
                for hh in hs:
                    yield from head(hh, main, s)
            gens = [chain(0, HEADS0), chain(1, HEADS1)]
            alive = [True, True]
            npre = 6 if main else 3
            for _ in range(npre):
                next(gens[0])
            gating()
            for _ in range(STAGGER - npre):
                try:
                    next(gens[0])
                except StopIteration:
                    alive[0] = False
                    break
            while any(alive):
                for i in range(2):
                    if alive[i]:
                        try:
                            next(gens[i])
                        except StopIteration:
                            alive[i] = False

        def add_x(fo):
            def ev(n, pb, ptag):
                tt("dve", x[:, fo, n * NT:(n + 1) * NT], x[:, fo, n * NT:(n + 1) * NT], pb, ALU.add,
                   [("x", fo), ptag], [("x", fo)])
            return ev

        def phase_out():
            for fo in range(KC):
                proj_to(wO[fo], KC, y, "y", add_x(fo))

        def phase_ffn():
            for g in range(NG):
                for jj in range(GRP):
                    j = g * GRP + jj
                    res = []
                    for half in range(2):
                        blk = 2 * j + half
                        stg, stag, finish = conv_stage(histf[:, blk, :], ("histf", blk), 2, "cf", 3 * blk, 3)

                        def ev(n, pb, ptag, stg=stg, stag=stag):
                            act(stg[:, 2 + n * NT:2 + (n + 1) * NT], pb, AF.Copy, [ptag], [stag])
                        proj_to(wU[blk], KC, h, "h", ev)
                        res.append(finish())
                    (gc_, gct, gci), (vc_, vct, vci) = res
                    act(gc_[:, 0:T], gc_[:, 0:T], AF.Silu, [gct], [gct])
                    tt("dve", y[:, jj, :], gc_[:, 0:T], vc_[:, 0:T], ALU.mult, [gct, vct], [("y", jj)])
                    freewk(gci)
                    freewk(vci)
                for fo in range(KC):
                    proj_to(wD[g * 16 + fo], GRP, y, "y", add_x(fo))

        def phase_ple(mp):
            for i in range(2):
                pt_, ptt, pti = getwk()
                dma("sp", pt_[:, 0:T], pin[i, :, mp * T:(mp + 1) * T], [], [ptt])
                cp("dve", pSb[:, i, :], pt_[:, 0:T], [ptt], [("p", i)])
                freewk(pti)
            for fo in range(KC):
                sg, sgt, sgi = getwk()

                def ev_g(n, pb, ptag):
                    act(sg[:, n * NT:(n + 1) * NT], pb, AF.Sigmoid, [ptag], [sgt])
                proj_to(wG[fo], KC, h, "h", ev_g)

                def ev_p(n, pb, ptag):
                    tt("dve", sg[:, n * NT:(n + 1) * NT], sg[:, n * NT:(n + 1) * NT], pb, ALU.mult, [sgt, ptag], [sgt])
                    tt("dve", x[:, fo, n * NT:(n + 1) * NT], x[:, fo, n * NT:(n + 1) * NT], sg[:, n * NT:(n + 1) * NT],
                       ALU.add, [("x", fo), sgt], [("x", fo)])
                proj_to(wP[fo], 2, pSb, "p", ev_p)
                freewk(sgi)

        for ps_i in range(NPASS_PRE + NPASS_MAIN):
            main = ps_i >= NPASS_PRE
            t0 = ps_i * T
            for fc in range(KC):
                dma("sp", x[:, fc, :], xin[fc, :, t0:t0 + T], [], [("x", fc)])
            rmsnorm("g1")
            if main:
                phase_A()
            run_heads(main)
            if not main:
                continue
            mp = ps_i - NPASS_PRE
            phase_out()
            rmsnorm("g2")
            phase_ffn()
            rmsnorm("g3")
            phase_ple(mp)
            if mp == 0:
                rmsnorm("gf", final_tok=(0, HALO))
            else:
                rmsnorm("gf", final_tok=(mp * T - HALO, 0))
        P.op("sp", lambda e: e.nop(), r=outtags)
        P.emit()
    return nc


def _blk(w, cols):
    K = w.shape[0]
    sub = w[:, cols]
    return np.ascontiguousarray(sub.reshape(K // 128, 128, sub.shape[1]).transpose(1, 0, 2))


def _prep_shared(inp):
    f32 = np.float32
    w_in = inp["w_in"][0]
    CW = 1024
    DN = 1024
    wA = np.stack([_blk(w_in, np.arange(s * CW + j * 128, s * CW + (j + 1) * 128))
                   for j in range(8) for s in (0, 2, 1)])
    base = 3 * CW
    wH = np.stack([_blk(w_in, np.arange(base + qi * DN + hh * 128, base + qi * DN + (hh + 1) * 128))
                   for hh in range(8) for qi in range(4)])
    wab = _blk(w_in, np.arange(base + 4 * DN, base + 4 * DN + 16))
    w_out = inp["w_out"][0]
    wO = np.stack([_blk(w_out, np.arange(fo * 128, (fo + 1) * 128)) for fo in range(16)])
    w_up = inp["w_up"][0]
    wU = np.stack([_blk(w_up, np.arange(half * DFF + j * 128, half * DFF + (j + 1) * 128))
                   for j in range(NHC) for half in range(2)])
    w_down = inp["w_down"][0]
    wD = np.stack([_blk(w_down[g * GRP * 128:(g + 1) * GRP * 128], np.arange(fo * 128, (fo + 1) * 128))
                   for g in range(NG) for fo in range(16)])
    wG = np.stack([_blk(inp["w_ple_gate"][0], np.arange(fo * 128, (fo + 1) * 128)) for fo in range(16)])
    wP = np.stack([_blk(inp["w_ple_proj"][0], np.arange(fo * 128, (fo + 1) * 128)) for fo in range(16)])

    par = np.zeros((128, NPAR), f32)

    def colblk(v):
        return v.reshape(-1, 128).T

    par[:, PO["g1"]:PO["g1"] + 16] = colblk(inp["norm_mix_g"][0])
    par[:, PO["g2"]:PO["g2"] + 16] = colblk(inp["norm_ffn_g"][0])
    par[:, PO["g3"]:PO["g3"] + 16] = colblk(inp["norm_ple_g"][0])
    par[:, PO["gf"]:PO["gf"] + 16] = colblk(inp["final_norm_g"])
    ca = inp["conv_a_w"][0]
    for j in range(8):
        par[:, PO["ca"] + 3 * j:PO["ca"] + 3 * j + 3] = ca[:, j * 128:(j + 1) * 128].T
    cq = inp["conv_qkv_w"][0]
    for b in range(24):
        par[:, PO["cq"] + 4 * b:PO["cq"] + 4 * b + 4] = cq[:, b * 128:(b + 1) * 128].T
    cf = inp["conv_ffn_w"][0]
    for j in range(NHC):
        for half in range(2):
            b = 2 * j + half
            c0 = half * DFF + j * 128
            par[:, PO["cf"] + 3 * b:PO["cf"] + 3 * b + 3] = cf[:, c0:c0 + 128].T
    par[:, PO["dng"]] = inp["dn_norm_g"][0]
    par[:, PO["dtb"]:PO["dtb"] + 8] = inp["dt_bias"][0][None, :]
    par[:, PO["alog"]:PO["alog"] + 8] = inp["a_log"][0][None, :]
    par[:, PO["ident"]:PO["ident"] + 128] = np.eye(128, dtype=f32)
    i = np.arange(64)
    par[0:64, PO["mus"]:PO["mus"] + 64] = (i[None, :] > i[:, None])
    par[0:64, PO["mui"]:PO["mui"] + 64] = (i[None, :] >= i[:, None])
    par[0:64, PO["mls"]:PO["mls"] + 64] = (i[None, :] < i[:, None])
    return dict(par=par, wA=wA, wH=wH, wab=wab, wO=wO, wU=wU, wD=wD, wG=wG, wP=wP)


def kernel(**inp):
    x = np.asarray(inp["x"], np.float32)
    p = np.asarray(inp["p"], np.float32)[0]
    inp = {k: np.asarray(v, np.float32) for k, v in inp.items()}
    shared = _prep_shared(inp)
    in_maps = []
    for c in range(8):
        b, half = c // 2, c % 2
        ntok = 2048 * (half + 1)
        xs = np.zeros((SEQ, D), np.float32)
        xs[SEQ - ntok:] = x[b, 0:ntok]
        xin = np.ascontiguousarray(xs.T.reshape(KC, 128, SEQ))
        pm = np.zeros((NMAIN, 256), np.float32)
        n_av = min(NMAIN, ntok)
        pm[NMAIN - n_av:] = p[b, ntok - n_av:ntok]
        pin = np.ascontiguousarray(pm.T.reshape(2, 128, NMAIN))
        m = dict(shared)
        m["xin"] = xin
        m["pin"] = pin
        in_maps.append(m)
    nc = build_nc()
    res = run_bass_kernel_spmd(nc, in_maps, core_ids=list(range(8)))
    outp = np.zeros((4, 4096, D), np.float32)
    for c in range(8):
        b, half = c // 2, c % 2
        o = res.results[c]["out"].reshape(D, NOUT).T
        outp[b, half * 2048:(half + 1) * 2048] = o
    return outp
```
